# Optimizing a Trainium2 kernel written in Bass

```python
import jax, jax.numpy as jnp
from jax import lax
import numpy as np

D_MODEL = 1024
BATCH = 1
SEQ = 16384
DEPTH = 4

D_MIX = D_MODEL
GLA_HEADS = 4
GLA_DV = 128
GLA_DK = 64
GLA_GATE_RANK = 16
GLA_GATE_NORMALIZER = 16.0
GLA_CHUNK = 64
GLA_WIDTH = GLA_HEADS * GLA_DV
LRU_WIDTH = D_MIX - GLA_WIDTH
LRU_BLOCKS = 8
LRU_BLOCK = LRU_WIDTH // LRU_BLOCKS
LRU_CONV = 4
LRU_C = 8.0
FFN_HIDDEN = 3 * D_MODEL
FFN_CONV = 3
EPS = 1e-6
Q_COLS = GLA_HEADS * GLA_DK
K_COLS = GLA_HEADS * GLA_DK
V_COLS = GLA_WIDTH
G_COLS = GLA_WIDTH
A_COLS = GLA_GATE_RANK
X_COLS = LRU_WIDTH
Y_COLS = LRU_WIDTH
D_IN = Q_COLS + K_COLS + V_COLS + G_COLS + A_COLS + X_COLS + Y_COLS
SPLITS = [Q_COLS, Q_COLS + K_COLS, Q_COLS + K_COLS + V_COLS,
          Q_COLS + K_COLS + V_COLS + G_COLS,
          Q_COLS + K_COLS + V_COLS + G_COLS + A_COLS,
          Q_COLS + K_COLS + V_COLS + G_COLS + A_COLS + X_COLS]

kernel_name = "hymba_gla_rglru_convffn_trunk"


def rmsnorm(x, g):
    xf = x.astype(jnp.float32)
    y = xf * lax.rsqrt(jnp.mean(xf * xf, axis=-1, keepdims=True) + EPS)
    return (y * g.astype(jnp.float32)).astype(x.dtype)


def causal_dwconv(x, w, b):
    K = w.shape[0]
    T = x.shape[1]
    xp = jnp.pad(x, ((0, 0), (K - 1, 0), (0, 0)))
    out = b
    for j in range(K):
        out = out + xp[:, j:j + T, :] * w[j]
    return out


def gla_heads(q, k, v, g_out, gate_lr, w2, b2, norm_g):
    f32 = jnp.float32
    Bsz, T, _ = q.shape
    N = T // GLA_CHUNK
    C = GLA_CHUNK
    log_alpha = jax.nn.log_sigmoid((gate_lr @ w2 + b2).astype(f32)) / GLA_GATE_NORMALIZER

    def chunks(t, d):
        return t.astype(f32).reshape(Bsz, N, C, GLA_HEADS, d).transpose(0, 3, 1, 2, 4)

    qc = chunks(q, GLA_DK) * (GLA_DK ** -0.5)
    kc = chunks(k, GLA_DK)
    vc = chunks(v, GLA_DV)
    gc = chunks(log_alpha, GLA_DK)
    b = jnp.cumsum(gc, axis=3)
    b_last = b[:, :, :, -1:, :]
    q_i = qc * jnp.exp(b)
    k_i = kc * jnp.exp(-b)
    k_dec = kc * jnp.exp(b_last - b)
    U = jnp.einsum('bhncd,bhnce->bhnde', k_dec, vc)
    decay = jnp.exp(b_last[:, :, :, 0, :])

    def step(S, inp):
        d_n, u_n = inp
        return d_n[..., None] * S + u_n, S

    S0 = jnp.zeros((Bsz, GLA_HEADS, GLA_DK, GLA_DV), f32)
    _, S_prev = lax.scan(step, S0, (jnp.moveaxis(decay, 2, 0), jnp.moveaxis(U, 2, 0)))
    S_prev = jnp.moveaxis(S_prev, 0, 2)
    mask = jnp.tril(jnp.ones((C, C), dtype=bool))
    A = jnp.where(mask, jnp.einsum('bhncd,bhnsd->bhncs', q_i, k_i), 0.0)
    o = (jnp.einsum('bhncs,bhnse->bhnce', A, vc)
         + jnp.einsum('bhncd,bhnde->bhnce', q_i, S_prev))
    o = o.transpose(0, 2, 3, 1, 4).reshape(Bsz, T, GLA_HEADS, GLA_DV)
    o = rmsnorm(o, norm_g)
    o = o * jax.nn.silu(g_out.astype(f32).reshape(Bsz, T, GLA_HEADS, GLA_DV))
    return o.reshape(Bsz, T, GLA_WIDTH).astype(q.dtype)


def rglru_heads(xr, xg, conv_w, conv_b, wa, ba, wx, bx, lam):
    f32 = jnp.float32
    Bsz, T, W = xr.shape
    xc = causal_dwconv(xr, conv_w, conv_b).astype(f32)
    xb = xc.reshape(Bsz, T, LRU_BLOCKS, LRU_BLOCK)
    r_gate = jax.nn.sigmoid(jnp.einsum('btnd,nde->btne', xb, wa.astype(f32)).reshape(Bsz, T, W) + ba)
    i_gate = jax.nn.sigmoid(jnp.einsum('btnd,nde->btne', xb, wx.astype(f32)).reshape(Bsz, T, W) + bx)
    log_a = -LRU_C * r_gate * jax.nn.softplus(-lam.astype(f32))
    a = jnp.exp(log_a)
    mult = jnp.sqrt(-jnp.expm1(2.0 * log_a))
    mult = jnp.where(jnp.arange(T)[None, :, None] == 0, 1.0, mult)
    u = mult * (i_gate * xc)

    def combine(left, right):
        a1, b1 = left
        a2, b2 = right
        return a1 * a2, a2 * b1 + b2

    _, h = lax.associative_scan(combine, (a, u), axis=1)
    y = h * jax.nn.gelu(xg.astype(f32))
    return y.astype(xr.dtype)


def conv_ffn(u, w_in, conv_w, conv_b, w_down):
    z = causal_dwconv(u @ w_in, conv_w, conv_b)
    a, gt = jnp.split(z, 2, axis=-1)
    return (jax.nn.gelu(a) * gt) @ w_down


def setup_inputs(seed: int = 0) -> dict:
    key = jax.random.key(seed)
    ks = jax.random.split(key, 24)
    f32 = jnp.float32
    nrm = lambda k, shape, s: jax.random.normal(k, shape, f32) * s
    u_lam = jax.random.uniform(ks[14], (DEPTH, LRU_WIDTH), f32, 0.9, 0.999)
    a0 = u_lam ** (1.0 / LRU_C)
    return {
        "x": jax.random.normal(ks[0], (BATCH, SEQ, D_MODEL), f32),
        "ln_mix": 1.0 + nrm(ks[1], (DEPTH, D_MODEL), 0.02),
        "w_in": nrm(ks[2], (DEPTH, D_MODEL, D_IN), D_MODEL ** -0.5),
        "gla_gate_w2": nrm(ks[3], (DEPTH, GLA_GATE_RANK, Q_COLS), GLA_GATE_RANK ** -0.5),
        "gla_gate_b": nrm(ks[4], (DEPTH, Q_COLS), 0.1),
        "gla_norm": 1.0 + nrm(ks[5], (DEPTH, GLA_DV), 0.02),
        "lru_conv_w": nrm(ks[6], (DEPTH, LRU_CONV, LRU_WIDTH), LRU_CONV ** -0.5),
        "lru_conv_b": nrm(ks[7], (DEPTH, LRU_WIDTH), 0.02),
        "lru_wa": nrm(ks[8], (DEPTH, LRU_BLOCKS, LRU_BLOCK, LRU_BLOCK), LRU_BLOCK ** -0.5),
        "lru_ba": nrm(ks[9], (DEPTH, LRU_WIDTH), 0.02),
        "lru_wx": nrm(ks[10], (DEPTH, LRU_BLOCKS, LRU_BLOCK, LRU_BLOCK), LRU_BLOCK ** -0.5),
        "lru_bx": nrm(ks[11], (DEPTH, LRU_WIDTH), 0.02),
        "lru_lambda": jnp.log(a0) - jnp.log1p(-a0),
        "w_out": nrm(ks[12], (DEPTH, D_MIX, D_MODEL), D_MIX ** -0.5),
        "ln_ffn": 1.0 + nrm(ks[13], (DEPTH, D_MODEL), 0.02),
        "ffn_w_in": nrm(ks[15], (DEPTH, D_MODEL, 2 * FFN_HIDDEN), D_MODEL ** -0.5),
        "ffn_conv_w": nrm(ks[16], (DEPTH, FFN_CONV, 2 * FFN_HIDDEN), FFN_CONV ** -0.5),
        "ffn_conv_b": nrm(ks[17], (DEPTH, 2 * FFN_HIDDEN), 0.02),
        "ffn_w_down": nrm(ks[18], (DEPTH, FFN_HIDDEN, D_MODEL), FFN_HIDDEN ** -0.5),
        "ln_final": 1.0 + nrm(ks[19], (D_MODEL,), 0.02),
    }


def reference(x, ln_mix, w_in, gla_gate_w2, gla_gate_b, gla_norm, lru_conv_w, lru_conv_b,
              lru_wa, lru_ba, lru_wx, lru_bx, lru_lambda, w_out, ln_ffn, ffn_w_in,
              ffn_conv_w, ffn_conv_b, ffn_w_down, ln_final):
    h = x
    for l in range(DEPTH):
        u = rmsnorm(h, ln_mix[l])
        proj = u @ w_in[l]
        q, k, v, g_out, gate_lr, lru_x, lru_g = jnp.split(proj, SPLITS, axis=-1)
        y_gla = gla_heads(q, k, v, g_out, gate_lr, gla_gate_w2[l], gla_gate_b[l], gla_norm[l])
        y_lru = rglru_heads(lru_x, lru_g, lru_conv_w[l], lru_conv_b[l], lru_wa[l], lru_ba[l],
                            lru_wx[l], lru_bx[l], lru_lambda[l])
        h = h + jnp.concatenate([y_gla, y_lru], axis=-1) @ w_out[l]
        u = rmsnorm(h, ln_ffn[l])
        h = h + conv_ffn(u, ffn_w_in[l], ffn_conv_w[l], ffn_conv_b[l], ffn_w_down[l])
    return rmsnorm(h, ln_final)
```

```python
import contextlib
from functools import partial

import numpy as np
import concourse.bass as bass
import concourse.mybir as mybir
from concourse.bass_utils import run_bass_kernel_spmd

F32 = mybir.dt.float32
BF16 = mybir.dt.bfloat16
ALU = mybir.AluOpType
AF = mybir.ActivationFunctionType

NCORES = 8
SEQ = 16384
D = 1024
T = SEQ // NCORES
TT = 512
NT = T // TT
KC = 8
DEPTH = 4
D_IN = 2576
QC, KCOL, VC, GC, AC, XC, YC = 0, 256, 512, 1024, 1536, 1552, 2064
FH = 3072
EPS = 1e-6
FG = 4
NG = 24 // FG

SP_LNM, SP_LNF, SP_GN, SP_LCW, SP_LCB, SP_BA, SP_BX, SP_LAM, SP_FCW, SP_FCB = (
    0, 8, 16, 17, 33, 37, 41, 45, 49, 193)
NSP = 241
CS_TRI, CS_MASK, CS_MSK, CS_SEL, CS_FF, CS_LNF = 0, 128, 384, 392, 400, 402
NCST = 410
PAYW = 528
GC1 = 1.5957691216057308
GC3 = 0.044715


class Buf:
    __slots__ = ("name", "w", "r")

    def __init__(self, name=""):
        self.name = name
        self.w = None
        self.r = []


class Op:
    __slots__ = ("eng", "fn", "deps", "signal", "event", "kind")

    def __init__(self, eng, fn, kind):
        self.eng = eng
        self.fn = fn
        self.kind = kind
        self.deps = []
        self.signal = False
        self.event = None


class KB:
    EPOCH = 30000

    def __init__(self, nc, es):
        self.nc = nc
        self.es = es
        self.ops = []
        self.engs = {"pe": nc.tensor, "act": nc.scalar, "dve": nc.vector,
                     "pool": nc.gpsimd, "sp": nc.sync}

    def sem(self, name):
        return self.es.enter_context(self.nc.semaphore(name))

    def _add(self, op, reads, writes):
        deps = []
        for b in reads:
            if b.w is not None:
                deps.append(b.w)
        for b in writes:
            if b.w is not None:
                deps.append(b.w)
            deps.extend(b.r)
        seen = set()
        for d in deps:
            if d is op or id(d) in seen:
                continue
            if d.eng == "pe" and op.eng == "pe" and d.kind == "c" and op.kind == "c":
                continue
            seen.add(id(d))
            op.deps.append(d)
            d.signal = True
        for b in writes:
            b.w = op
            b.r = []
        for b in reads:
            if b.w is not op:
                b.r.append(op)
        self.ops.append(op)
        return op

    def op(self, eng, fn, reads=(), writes=()):
        return self._add(Op(eng, fn, "c"), list(reads), list(writes))

    def dma(self, eng, fn, reads=(), writes=()):
        o = self._add(Op(eng, fn, "dma"), list(reads), list(writes))
        o.signal = True
        return o

    def cc(self, fn, reads=(), writes=()):
        o = self._add(Op("pool", fn, "cc"), list(reads), list(writes))
        o.signal = True
        return o

    def emit(self, final_ops=()):
        nc = self.nc
        ndma = 12
        esem = {e: [] for e in self.engs}
        ecnt = {e: 0 for e in self.engs}
        dsem = {e: [(self.sem(f"d_{e}_{i}"), [0]) for i in range(ndma)] for e in ("pool", "sp")}
        drr = {e: 0 for e in dsem}
        ccsem = self.sem("cc")
        cccnt = 0
        last_cc = None
        waited = {e: {} for e in self.engs}

        def wait_on(eng, evs, ins_fn, embed=True):
            need = []
            for s, (sh, v) in evs.items():
                if waited[eng].get(s, 0) < v:
                    need.append((sh, v))
                    waited[eng][s] = v
            for (sh, v) in (need[1:] if embed else need):
                self.engs[eng].wait_ge(sh, v)
            ins = ins_fn()
            if need and embed:
                ins._wait_ge(need[0][0], need[0][1])
            return ins

        for op in self.ops:
            evs = {}
            for d in op.deps:
                sh, v = d.event
                k = id(sh)
                if k not in evs or evs[k][1] < v:
                    evs[k] = (sh, v)
            eng = op.eng
            if op.kind == "c":
                ins = wait_on(eng, evs, op.fn)
                if op.signal:
                    if ecnt[eng] % self.EPOCH == 0:
                        esem[eng].append(self.sem(f"e_{eng}_{len(esem[eng])}"))
                    ecnt[eng] += 1
                    v = (ecnt[eng] - 1) % self.EPOCH + 1
                    ins.then_inc(esem[eng][-1], 1)
                    op.event = (esem[eng][-1], v)
            elif op.kind == "dma":
                i = drr[eng] % ndma
                drr[eng] += 1
                sh, cnt = dsem[eng][i]
                if cnt[0] > 0:
                    k = id(sh)
                    if k not in evs or evs[k][1] < cnt[0] * 16:
                        evs[k] = (sh, cnt[0] * 16)
                ins = wait_on(eng, evs, op.fn)
                cnt[0] += 1
                ins.then_inc(sh, 16)
                op.event = (sh, cnt[0] * 16)
            else:
                if last_cc is not None:
                    evs[id(last_cc[0])] = last_cc
                ins = wait_on(eng, evs, op.fn, embed=False)
                cccnt += 1
                csem = self.sem(f"cc{cccnt}")
                ins.then_inc(csem)
                op.event = (csem, 1)
                last_cc = op.event
        evs = {}
        for d in final_ops:
            sh, v = d.event
            k = id(sh)
            if k not in evs or evs[k][1] < v:
                evs[k] = (sh, v)
        for k, (sh, v) in evs.items():
            nc.sync.wait_ge(sh, v)


class TV:
    __slots__ = ("ap", "buf")

    def __init__(self, ap, buf=None):
        self.ap = ap
        self.buf = buf if buf is not None else Buf()

    def __getitem__(self, idx):
        return TV(self.ap[idx], self.buf)


class Ring:
    def __init__(self, tvs):
        self.tvs = tvs
        self.i = 0

    def next(self):
        t = self.tvs[self.i % len(self.tvs)]
        self.i += 1
        return t


class _Stop(Exception):
    pass


def build_program(L, final_norm, stop=99):
    nc = bass.Bass("TRN2", target_bir_lowering=False)
    es = contextlib.ExitStack()
    kb = KB(nc, es)

    dt = nc.dram_tensor
    hin_d = dt("hin", [128, KC, T], F32, kind="ExternalInput").ap()
    halo_d = dt("halo", [128, KC * 3], F32, kind="ExternalInput").ap()
    w_in_d = dt("w_in", [L, D, D_IN], F32, kind="ExternalInput").ap()
    w_out_d = dt("w_out", [L, D, D], F32, kind="ExternalInput").ap()
    f_in_d = dt("f_in", [L, D, 2 * FH], F32, kind="ExternalInput").ap()
    f_dn_d = dt("f_dn", [L, FH, D], F32, kind="ExternalInput").ap()
    spar_d = dt("spar", [L, 128, NSP], F32, kind="ExternalInput").ap()
    lruw_d = dt("lruw", [L, 128, 1024], F32, kind="ExternalInput").ap()
    w2a_d = dt("w2a", [L, 32, 256], F32, kind="ExternalInput").ap()
    cst_d = dt("cst", [128, NCST], F32, kind="ExternalInput").ap()
    hout_d = dt("hout", [128, KC, T], F32, kind="ExternalOutput").ap()
    ib1 = dt("ib1", [128, 520], F32)
    ob1 = dt("ob1", [128 * NCORES, 520], F32)
    ib3 = dt("ib3", [128, 8], F32)
    ob3 = dt("ob3", [128 * NCORES, 8], F32)
    ib3_b, ob3_b = Buf("ib3"), Buf("ob3")
    ib2 = dt("ib2", [128, 24], F32)
    ob2 = dt("ob2", [128 * NCORES, 24], F32)
    ib1_b, ob1_b, ib2_b, ob2_b = Buf("ib1"), Buf("ob1"), Buf("ib2"), Buf("ob2")

    cap = (nc.sbuf_bytes_remaining // 64) * 64 - 64
    arena = nc.alloc_sbuf_tensor("arena", [128, cap // 2], BF16)
    pos = [0]

    def alloc(nbytes):
        o = pos[0]
        pos[0] += (nbytes + 63) // 64 * 64
        assert pos[0] <= cap, f"SBUF overflow {pos[0]} > {cap}"
        return o

    def view(off, nbytes, dtype, parts=128):
        ap = arena[0:parts, off // 2:(off + nbytes) // 2]
        return ap.bitcast(F32) if dtype == F32 else ap

    def mk(shape, dtype, off=None, parts=128):
        n = int(np.prod(shape[1:])) * (4 if dtype == F32 else 2)
        if off is None:
            off = alloc(n)
        ap = view(off, n, dtype, parts)
        if len(shape) == 3:
            ap = ap.rearrange("p (a b) -> p a b", a=shape[1])
        return ap

    H_ap = mk([128, KC, T], F32)
    U_ap = mk([128, KC, T], BF16)
    s1_off = pos[0]
    S1_ap = mk([128, 4, T], BF16)
    S2_ap = mk([128, 2, T], BF16)
    S3_ap = mk([128, 4, T], BF16)
    S4_ap = mk([128, 4, T], BF16)
    tsl = lambda i: slice(i * TT, (i + 1) * TT)
    H = [[TV(H_ap[:, kc, tsl(i)]) for i in range(NT)] for kc in range(KC)]
    U = [[TV(U_ap[:, kc, tsl(i)]) for i in range(NT)] for kc in range(KC)]
    S1 = [[TV(S1_ap[:, h, tsl(i)]) for i in range(NT)] for h in range(4)]
    S2 = [[TV(S2_ap[:, p, tsl(i)]) for i in range(NT)] for p in range(2)]
    S3 = [[TV(S3_ap[:, c, tsl(i)]) for i in range(NT)] for c in range(4)]
    S4 = [[TV(S4_ap[:, c, tsl(i)]) for i in range(NT)] for c in range(4)]
    WF = []
    for par in range(2):
        base = s1_off + par * 24576
        fin = view(base, 16384, BF16).rearrange("p (a b) -> p a b", a=KC)
        fdn = view(base + 16384, 8192, BF16).rearrange("p (a b) -> p a b", a=FG)
        WF.append((fin, fdn))
    store_bufs = ([t.buf for r in S1 for t in r] + [t.buf for r in S2 for t in r]
                  + [t.buf for r in S3 for t in r] + [t.buf for r in S4 for t in r])
    wf_guard = [Buf("wf0"), Buf("wf1")]

    sp = TV(mk([128, NSP], F32))
    cst = TV(mk([128, NCST], F32))
    ones = TV(mk([128, 512], BF16))
    ones_bf = ones[:, 0:128]
    glr = TV(mk([32, 512], F32, parts=32))
    w2a = TV(mk([32, 256], F32, parts=32))
    lruw = TV(mk([128, 1024], F32))
    Sf = [TV(mk([128, 256], F32)) for _ in range(2)]
    Sbf = [TV(mk([128, 256], BF16)) for _ in range(2)]
    smalls = mk([128, 256], F32)
    _sp = [0]

    def small(n):
        o = _sp[0]
        _sp[0] += n
        assert _sp[0] <= 256
        return TV(smalls[:, o:o + n])

    offs = [small(8) for _ in range(2)]
    eoff = [small(4) for _ in range(2)]
    hst, Pst, hin = small(4), small(4), small(4)
    clc = small(8)
    xtail = [small(4) for _ in range(4)]
    tl = small(24)
    sm = [small(8) for _ in range(6)]
    epsc = small(1)
    dummy = small(1)
    dummy2 = small(1)
    zt_ap = mk([128, 96], F32)
    ztail = [TV(zt_ap[:, 2 * z:2 * z + 2]) for z in range(48)]
    uh = TV(mk([128, KC, 4], BF16))
    kz = [TV(mk([128, 512], BF16)) for _ in range(2)]

    w_base = pos[0]
    n_f512, n_b512, n_f256, n_b256, n_f514 = 6, 5, 2, 2, 3
    work_bytes = n_f512 * 2048 + n_b512 * 1024 + n_f256 * 1024 + n_b256 * 512 + n_f514 * 2112
    work_off = cap - work_bytes - 64
    wreg_bytes = work_off - w_base
    assert wreg_bytes >= 12 * 1024, f"weights area too small: {wreg_bytes}"
    wpos = [work_off]

    def walloc(shape, dtype):
        n = int(np.prod(shape[1:])) * (4 if dtype == F32 else 2)
        o = wpos[0]
        wpos[0] += (n + 63) // 64 * 64
        assert wpos[0] <= cap
        return view(o, n, dtype)

    f512 = Ring([TV(walloc([128, 512], F32)) for _ in range(n_f512)])
    b512 = Ring([TV(walloc([128, 512], BF16)) for _ in range(n_b512)])
    f256 = Ring([TV(walloc([128, 256], F32)) for _ in range(n_f256)])
    b256 = Ring([TV(walloc([128, 256], BF16)) for _ in range(n_b256)])
    f514 = Ring([TV(walloc([128, 528], F32)) for _ in range(n_f514)])
    wreg = Buf("wreg")
    n_ffx = 4
    ffx = [TV(view(w_base + 2048 * k, 2048, F32)) for k in range(n_ffx)]
    n_fzx = min(2, (wreg_bytes - 2048 * n_ffx) // 2112)
    fzx = [TV(view(w_base + 2048 * n_ffx + 2112 * k, 2112, F32)) for k in range(n_fzx)]
    ffx_bufs = [t.buf for t in ffx] + [t.buf for t in fzx]
    ffr = Ring(f512.tvs + ffx)
    fzr = Ring(f514.tvs + fzx)

    PS = [TV(nc.alloc_psum_tensor(f"ps{i}", [128, 512], F32)[:, :]) for i in range(8)]
    PO = PS[0:2]
    psr = Ring(PS[2:8])
    psf = Ring(PS[0:8])

    def B(xs):
        out = []
        for x in xs:
            if x is None or isinstance(x, (int, float)):
                continue
            if isinstance(x, TV):
                out.append(x.buf)
            elif isinstance(x, Buf):
                out.append(x)
            else:
                out.extend(B(x))
        return out

    def A(x):
        return x.ap if isinstance(x, TV) else x

    def MM(out, lhsT, rhs, start=True, stop=True, xr=()):
        kb.op("pe", partial(nc.tensor.matmul, out.ap, lhsT.ap, rhs.ap, start=start, stop=stop),
              B([lhsT, rhs, xr]), B([out]))

    def ACT(out, in_, func, bias=0.0, scale=1.0, xr=()):
        kb.op("act", partial(nc.scalar.activation, out.ap, in_.ap, func, bias=A(bias), scale=A(scale)),
              B([in_, bias, scale, xr]), B([out]))

    def TT_(eng, out, in0, in1, op, xr=()):
        e = nc.vector if eng == "dve" else nc.gpsimd
        kb.op(eng, partial(e.tensor_tensor, out.ap, in0.ap, in1.ap, op), B([in0, in1, xr]), B([out]))

    def TS(eng, out, in0, s1, s2, op0, op1=None, xr=()):
        e = nc.vector if eng == "dve" else nc.gpsimd
        if op1 is None:
            fn = partial(e.tensor_scalar, out.ap, in0.ap, A(s1), None, op0)
        else:
            fn = partial(e.tensor_scalar, out.ap, in0.ap, A(s1), A(s2), op0, op1)
        kb.op(eng, fn, B([in0, s1, s2, xr]), B([out]))

    def STT(out, in0, s, in1, op0, op1, xr=()):
        kb.op("dve", partial(nc.vector.scalar_tensor_tensor, out.ap, in0.ap, A(s), in1.ap, op0, op1),
              B([in0, s, in1, xr]), B([out]))

    def SCAN(out, d0, d1, init, op0, op1, xr=()):
        kb.op("dve", partial(nc.vector.tensor_tensor_scan, out.ap, d0.ap, d1.ap, A(init), op0, op1),
              B([d0, d1, init, xr]), B([out]))

    def CP(eng, out, in_, xr=()):
        if eng == "act":
            kb.op("act", partial(nc.scalar.copy, out.ap, in_.ap), B([in_, xr]), B([out]))
        else:
            e = nc.vector if eng == "dve" else nc.gpsimd
            kb.op(eng, partial(e.tensor_copy, out.ap, in_.ap), B([in_, xr]), B([out]))

    def MEMSET(out, val, eng="dve", xw=()):
        e = nc.vector if eng == "dve" else nc.gpsimd
        kb.op(eng, partial(e.memset, out.ap, val), [], B([out, xw]))

    def DMA(q, out_ap, in_ap, reads, writes):
        e = nc.gpsimd if q == "pool" else nc.sync
        return kb.dma(q, partial(e.dma_start, out=out_ap, in_=in_ap), B(reads), B(writes))

    def GUARD(bufs):
        MEMSET(dummy, 0.0, eng="pool", xw=bufs)

    def spc(col, n=1):
        return sp[:, col:col + n]

    def load_cols(dst3, bufs_k, src2d, c0, c1, dc0):
        n = c1 - c0
        for k in range(len(bufs_k)):
            DMA("pool", dst3[:, k, dc0:dc0 + n], src2d[k * 128:(k + 1) * 128, c0:c1], [], [bufs_k[k]])

    def wview(ncols, nk=KC, off=0):
        ap = view(w_base + off, nk * ncols * 2, BF16).rearrange("p (a b) -> p a b", a=nk)
        assert off + nk * ncols * 2 <= wreg_bytes, (off, nk, ncols, wreg_bytes)
        return ap

    def rmsnorm(src, gcol, dst, n, nfeat=D):
        ss = psr.next()
        for kc in range(KC):
            sq = b512.next()
            ACT(sq[:, 0:n], src(kc), AF.Square)
            MM(ss[:, 0:n], ones_bf, sq[:, 0:n], start=(kc == 0), stop=(kc == KC - 1))
        t = f512.next()
        ACT(t[:, 0:n], ss[:, 0:n], AF.Ln, bias=epsc, scale=1.0 / nfeat)
        ACT(t[:, 0:n], t[:, 0:n], AF.Exp, scale=-0.5)
        for kc in range(KC):
            STT(dst(kc), src(kc), gcol(kc), t[:, 0:n], ALU.mult, ALU.mult)

    def allgather(ib, ib_b, ob, ob_b):
        kb.cc(partial(nc.gpsimd.collective_compute, "AllGather", ALU.bypass,
                      replica_groups=[list(range(NCORES))], ins=[ib.ap().opt()], outs=[ob.ap().opt()]),
              [ib_b], [ob_b])

    def halo_send(ntok):
        pay = f514.next()
        MEMSET(pay[:, 0:24], 0.0)
        for kc in range(KC):
            CP("act", pay[:, kc * ntok:(kc + 1) * ntok], H[kc][NT - 1][:, TT - ntok:TT])
        DMA("pool", ib2.ap(), pay.ap[:, 0:24], [pay], [ib2_b])
        allgather(ib2, ib2_b, ob2, ob2_b)

    def halo_recv(ntok, gcol, first_src=None):
        if first_src is None:
            g2 = f514.next()
            g2v = g2.ap[:, 0:NCORES * 24].rearrange("p (c n) -> p c n", c=NCORES)
            DMA("pool", g2v, ob2.ap().rearrange("(c p) n -> p c n", p=128), [ob2_b], [g2])
            MEMSET(tl, 0.0)
            for j in range(NCORES):
                STT(tl[:, 0:KC * ntok], TV(g2v[:, j, 0:KC * ntok], g2.buf), cst[:, CS_SEL + j:CS_SEL + j + 1],
                    tl[:, 0:KC * ntok], ALU.mult, ALU.add)
        else:
            DMA("sp", tl.ap[:, 0:KC * ntok], first_src[:, 0:KC * ntok], [], [tl])
        rmsnorm(lambda kc: tl[:, kc * ntok:(kc + 1) * ntok], gcol,
                lambda kc: TV(uh.ap[:, kc, 0:ntok], uh.buf), ntok)

    MEMSET(epsc, EPS)
    DMA("sp", cst.ap, cst_d, [], [cst])
    for kc in range(KC):
        for i in range(NT):
            DMA("sp", H[kc][i].ap, hin_d[:, kc, tsl(i)], [], [H[kc][i]])
    MEMSET(ones, 1.0)
    MEMSET(glr, 1.0)
    MEMSET(kz[0], 0.0)
    MEMSET(kz[1], 0.0)
    trigt = cst[:, CS_TRI:CS_TRI + 128]
    mask2 = cst[:, CS_MASK:CS_MASK + 256]
    msk = cst[:, CS_MSK:CS_MSK + 8]

    def CK(k):
        if stop == k:
            raise _Stop()

    try:
      for l in range(L):
          DMA("sp", sp.ap, spar_d[l], [], [sp])
          DMA("sp", lruw.ap, lruw_d[l], [], [lruw])
          DMA("sp", w2a.ap, w2a_d[l], [], [w2a])
          ACT(sm[0][:, 0:4], spc(SP_LAM, 4), AF.Exp, scale=-1.0)
          ACT(sm[0][:, 0:4], sm[0][:, 0:4], AF.Ln, bias=1.0)
          TS("dve", clc[:, 0:4], sm[0][:, 0:4], -8.0, None, ALU.mult)
          TS("dve", clc[:, 4:8], sm[0][:, 0:4], -16.0, None, ALU.mult)


          CK(1)
          for i in range(NT):
              rmsnorm(lambda kc: H[kc][i], lambda kc: spc(SP_LNM + kc), lambda kc: U[kc][i], TT)

          CK(2)
          for p in range(2):
              GUARD([wreg])
              WA = wview(528)
              wab = [Buf() for _ in range(KC)]
              wib = w_in_d[l]
              load_cols(WA, wab, wib, QC + 128 * p, QC + 128 * p + 128, 0)
              wkb = [Buf() for _ in range(KC)]
              load_cols(WA, wkb, wib, KCOL + 128 * p, KCOL + 128 * p + 128, 128)
              wvb = [Buf() for _ in range(KC)]
              load_cols(WA, wvb, wib, VC + 256 * p, VC + 256 * p + 256, 256)
              wgb = [Buf() for _ in range(KC)]
              load_cols(WA, wgb, wib, AC, AC + 16, 512)
              voff = KC * 528 * 2
              v_ap = view(w_base + voff, 4 * 256 * 2, BF16).rearrange("p (a b) -> p a b", a=4)
              kd_ap = view(w_base + voff + 2048, 4 * 128 * 2, BF16).rearrange("p (a b) -> p a b", a=4)
              assert voff + 3072 <= wreg_bytes
              vS = [TV(v_ap[:, s, :]) for s in range(4)]
              kdS = [TV(kd_ap[:, s, :]) for s in range(4)]
              MEMSET(Sf[p], 0.0)
              MEMSET(Sbf[p], 0.0)
              MEMSET(offs[p], 0.0)
              for i in range(NT):
                  pg = psr.next()
                  for kc in range(KC):
                      MM(pg[0:16, :], TV(WA[:, kc, 512:528], wgb[kc]), U[kc][i], start=(kc == 0), stop=(kc == KC - 1),
                         xr=[wreg])
                  CP("act", glr[0:16, :], pg[0:16, :])
                  CK(2.1)
                  for s in range(4):
                      ssl = slice(s * 128, (s + 1) * 128)
                      pxg = psr.next()
                      MM(pxg[:, 0:128], glr[0:17, ssl], w2a[0:17, 128 * p:128 * p + 128])
                      e2 = f256.next()
                      ACT(e2[:, 0:128], pxg[:, 0:128], AF.Exp, scale=-1.0)
                      ACT(e2[:, 0:128], e2[:, 0:128], AF.Ln, bias=1.0)
                      pc = psr.next()
                      MM(pc[:, 0:128], trigt, e2[:, 0:128])
                      edec = f256.next()
                      ACT(edec[:, 0:128], pc[:, 0:128], AF.Exp, scale=-1.0 / 16.0)
                      pkt = psr.next()
                      for kc in range(KC):
                          MM(pkt[:, 0:128], U[kc][i][:, ssl], TV(WA[:, kc, 128:256], wkb[kc]),
                             start=(kc == 0), stop=(kc == KC - 1), xr=[wreg])
                      TT_("dve", kdS[s], pkt[:, 0:128], edec[:, 0:128], ALU.mult, xr=[wreg])
                      pv = psr.next()
                      for kc in range(KC):
                          MM(pv[:, 0:256], U[kc][i][:, ssl], TV(WA[:, kc, 256:512], wvb[kc]),
                             start=(kc == 0), stop=(kc == KC - 1), xr=[wreg])
                      CP("act", vS[s], pv[:, 0:256], xr=[wreg])
                  CK(2.2)
                  pxt = psr.next()
                  MM(pxt, w2a[0:17, 128 * p:128 * p + 128], glr[0:17, :])
                  e1 = f512.next()
                  ACT(e1, pxt, AF.Exp, scale=-1.0)
                  spf = f512.next()
                  ACT(spf, e1, AF.Ln, bias=1.0)
                  cs = f512.next()
                  for s in range(4):
                      ssl = slice(s * 128, (s + 1) * 128)
                      SCAN(cs[:, ssl], ones[:, 0:128], spf[:, ssl], 0.0, ALU.mult, ALU.add)
                  eq = f512.next()
                  ACT(eq, cs, AF.Exp, scale=-1.0 / 16.0)
                  ek = e1
                  ACT(ek, cs, AF.Exp, scale=1.0 / 16.0)
                  for s in range(4):
                      TT_("dve", offs[p][:, s + 1:s + 2], offs[p][:, s:s + 1], cs[:, 128 * s + 127:128 * s + 128], ALU.add)
                  ACT(eoff[p], offs[p][:, 0:4], AF.Exp, scale=-1.0 / 16.0)
                  CK(2.3)
                  pq = psr.next()
                  for kc in range(KC):
                      MM(pq, TV(WA[:, kc, 0:128], wab[kc]), U[kc][i], start=(kc == 0), stop=(kc == KC - 1), xr=[wreg])
                  qi = b512.next()
                  STT(qi, pq, 0.125, eq, ALU.mult, ALU.mult)
                  pk = psr.next()
                  for kc in range(KC):
                      MM(pk, TV(WA[:, kc, 128:256], wkb[kc]), U[kc][i], start=(kc == 0), stop=(kc == KC - 1), xr=[wreg])
                  TT_("dve", kz[0][0:64, :], pk[0:64, :], ek[0:64, :], ALU.mult)
                  TT_("dve", kz[1][64:128, :], pk[64:128, :], ek[64:128, :], ALU.mult)
                  for s in range(4):
                      ssl = slice(s * 128, (s + 1) * 128)
                      TS("dve", S2[p][i][:, ssl], qi[:, ssl], eoff[p][:, s:s + 1], None, ALU.mult)
                  CP("dve", offs[p][:, 0:1], offs[p][:, 4:5])
                  CK(2.4)
                  for s in range(4):
                      ssl = slice(s * 128, (s + 1) * 128)
                      pA = psr.next()
                      for a in range(2):
                          MM(pA[:, 128 * a:128 * a + 128], kz[a][:, ssl], qi[:, ssl])
                      At = b256.next()
                      TT_("dve", At, pA[:, 0:256], mask2, ALU.mult)
                      for a in range(2):
                          MM(PO[a][:, ssl], vS[s][:, 128 * a:128 * a + 128], At[:, 128 * a:128 * a + 128],
                             start=True, stop=False, xr=[wreg])
                          MM(PO[a][:, ssl], Sbf[p][:, 128 * a:128 * a + 128], qi[:, ssl], start=False, stop=True)
                      pU = psr.next()
                      MM(pU[:, 0:256], kdS[s], vS[s], xr=[wreg])
                      STT(Sf[p], Sf[p], eq[:, 128 * s + 127:128 * s + 128], pU[:, 0:256], ALU.mult, ALU.add)
                      CP("act", Sbf[p][0:64, 0:128], Sf[p][0:64, 0:128])
                      CP("act", Sbf[p][64:128, 128:256], Sf[p][64:128, 128:256])
                  CK(2.5)
                  for a in range(2):
                      CP("act", S1[2 * p + a][i], PO[a])

          CK(3)
          pay = f514.next()
          MEMSET(pay[:, 512:520], 0.0)
          for p in range(2):
              CP("act", pay[:, 256 * p:256 * p + 256], Sf[p])
              CP("act", pay[:, 512 + p:513 + p], offs[p][:, 0:1])
          DMA("pool", ib1.ap(), pay.ap[:, 0:520], [pay], [ib1_b])
          allgather(ib1, ib1_b, ob1, ob1_b)
          halo_recv(3, lambda kc: spc(SP_LNM + kc), first_src=(halo_d if l == 0 else None))
          GUARD([wreg])
          WX = wview(512)
          wxb = [Buf() for _ in range(KC)]
          load_cols(WX, wxb, w_in_d[l], XC, XC + 512, 0)
          for c in range(4):
              px3 = psr.next()
              for kc in range(KC):
                  MM(px3[:, 0:3], TV(WX[:, kc, 128 * c:128 * c + 128], wxb[kc]), TV(uh.ap[:, kc, 0:3], uh.buf),
                     start=(kc == 0), stop=(kc == KC - 1), xr=[wreg])
              CP("act", xtail[c][:, 0:3], px3[:, 0:3])
          for i in range(NT):
              for c in range(4):
                  px = psr.next()
                  for kc in range(KC):
                      MM(px, TV(WX[:, kc, 128 * c:128 * c + 128], wxb[kc]), U[kc][i],
                         start=(kc == 0), stop=(kc == KC - 1), xr=[wreg])
                  xs = f514.next()
                  CP("act", xs[:, 3:515], px)
                  CP("pool", xs[:, 0:3], xtail[c][:, 0:3])
                  CP("pool", xtail[c][:, 0:3], xs[:, 512:515])
                  xc = f512.next()
                  cw = lambda j: spc(SP_LCW + 4 * c + j)
                  ACT(xc, xs[:, 3:515], AF.Identity, bias=spc(SP_LCB + c), scale=cw(3))
                  for j in (2, 1, 0):
                      STT(xc, xs[:, j:j + 512], cw(j), xc, ALU.mult, ALU.add)
                  pa = psr.next()
                  MM(pa, lruw[:, (2 * c) * 128:(2 * c + 1) * 128], xc)
                  pxg = psr.next()
                  MM(pxg, lruw[:, (2 * c + 1) * 128:(2 * c + 2) * 128], xc)
                  r = f512.next()
                  ACT(r, pa, AF.Sigmoid, bias=spc(SP_BA + c))
                  ig = f512.next()
                  ACT(ig, pxg, AF.Sigmoid, bias=spc(SP_BX + c))
                  av = f512.next()
                  ACT(av, r, AF.Exp, scale=clc[:, c:c + 1])
                  ACT(r, r, AF.Exp, scale=clc[:, 4 + c:5 + c])
                  ACT(r, r, AF.Sqrt, bias=1.0, scale=-1.0)
                  if i == 0:
                      TS("dve", r[:, 0:1], r[:, 0:1], cst[:, CS_FF + 1:CS_FF + 2], cst[:, CS_FF:CS_FF + 1],
                         ALU.mult, ALU.add)
                  TT_("dve", ig, ig, xc, ALU.mult)
                  TT_("dve", ig, ig, r, ALU.mult)
                  hl = f512.next()
                  SCAN(hl, av, ig, 0.0 if i == 0 else hst[:, c:c + 1], ALU.mult, ALU.add)
                  Pp = f512.next()
                  SCAN(Pp, av, ones, 1.0 if i == 0 else Pst[:, c:c + 1], ALU.mult, ALU.mult)
                  CP("dve", hst[:, c:c + 1], hl[:, 511:512])
                  CP("dve", Pst[:, c:c + 1], Pp[:, 511:512])
                  CP("pool", S3[c][i], hl)
                  CP("pool", S4[c][i], Pp)

          CK(4)
          pay3 = sm[0]
          CP("act", pay3[:, 0:4], hst)
          CP("act", pay3[:, 4:8], Pst)
          DMA("pool", ib3.ap(), pay3.ap, [pay3], [ib3_b])
          allgather(ib3, ib3_b, ob3, ob3_b)
          ob1v = ob1.ap().rearrange("(c p) n -> p c n", p=128)
          gs = f514.next()
          gsv = gs.ap[:, 0:NCORES * 8].rearrange("p (c n) -> p c n", c=NCORES)
          DMA("pool", gsv, ob1v[:, :, 512:520], [ob1_b], [gs])
          for p in range(2):
              GUARD([wreg])
              g1_ap = view(w_base, NCORES * 256 * 4, F32).rearrange("p (c n) -> p c n", c=NCORES)
              assert NCORES * 1024 <= wreg_bytes
              g1b = Buf()
              kb.dma("pool", partial(nc.gpsimd.dma_start, out=g1_ap, in_=ob1v[:, :, 256 * p:256 * p + 256]),
                     [ob1_b, wreg], [g1b])
              csm = sm[1]
              TT_("dve", csm, TV(gsv[:, :, p], gs.buf), msk, ALU.mult)
              dm = sm[2 + p]
              ACT(dm, csm, AF.Exp, scale=-1.0 / 16.0)
              MEMSET(Sf[p], 0.0)
              for j in range(NCORES):
                  t = f256.next()
                  TS("dve", t, TV(g1_ap[:, j, :], g1b), msk[:, j:j + 1], None, ALU.mult, xr=[wreg])
                  STT(Sf[p], Sf[p], dm[:, j:j + 1], t, ALU.mult, ALU.add)
              CP("act", Sbf[p][0:64, 0:128], Sf[p][0:64, 0:128])
              CP("act", Sbf[p][64:128, 128:256], Sf[p][64:128, 128:256])
          CK(5)
          GUARD([wreg])
          WG = wview(512)
          wgb2 = [Buf() for _ in range(KC)]
          load_cols(WG, wgb2, w_in_d[l], GC, GC + 512, 0)
          for i in range(NT):
              for hd in range(4):
                  p, a = hd // 2, hd % 2
                  pgt = psr.next()
                  for kc in range(KC):
                      MM(pgt, TV(WG[:, kc, 128 * hd:128 * hd + 128], wgb2[kc]), U[kc][i],
                         start=(kc == 0), stop=(kc == KC - 1), xr=[wreg])
                  sg = f512.next()
                  ACT(sg, pgt, AF.Sigmoid)
                  pcr = psr.next()
                  MM(pcr, Sbf[p][:, 128 * a:128 * a + 128], S2[p][i])
                  o = f512.next()
                  TT_("dve", o, pcr, S1[hd][i], ALU.add)
                  osq = b512.next()
                  ACT(osq, o, AF.Square)
                  pss = psr.next()
                  MM(pss, ones_bf, osq)
                  t = f512.next()
                  ACT(t, pss, AF.Ln, bias=epsc, scale=1.0 / 128.0)
                  ACT(t, t, AF.Exp, scale=-0.5)
                  STT(o, o, spc(SP_GN), t, ALU.mult, ALU.mult)
                  TT_("dve", o, o, sg, ALU.mult)
                  TT_("dve", S1[hd][i], o, pgt, ALU.mult)
          CK(6)
          g3 = f514.next()
          g3v = g3.ap[:, 0:NCORES * 8].rearrange("p (c n) -> p c n", c=NCORES)
          DMA("pool", g3v, ob3.ap().rearrange("(c p) n -> p c n", p=128), [ob3_b], [g3])
          G3 = lambda j, a, b: TV(g3v[:, j, a:b], g3.buf)
          MEMSET(hin, 0.0)
          for j in range(NCORES):
              pm = sm[4]
              TS("dve", pm[:, 0:4], G3(j, 4, 8), -1.0, msk[:, j:j + 1], ALU.add, ALU.mult)
              TS("dve", pm[:, 0:4], pm[:, 0:4], 1.0, None, ALU.add)
              hm = sm[5]
              TS("dve", hm[:, 0:4], G3(j, 0, 4), msk[:, j:j + 1], None, ALU.mult)
              TT_("dve", hin, hin, pm[:, 0:4], ALU.mult)
              TT_("dve", hin, hin, hm[:, 0:4], ALU.add)
          GUARD([wreg])
          WY = wview(512)
          wyb = [Buf() for _ in range(KC)]
          load_cols(WY, wyb, w_in_d[l], YC, YC + 512, 0)
          for i in range(NT):
              for c in range(4):
                  py = psr.next()
                  for kc in range(KC):
                      MM(py, TV(WY[:, kc, 128 * c:128 * c + 128], wyb[kc]), U[kc][i],
                         start=(kc == 0), stop=(kc == KC - 1), xr=[wreg])
                  gy = f512.next()
                  ACT(gy, py, AF.Square, scale=GC3 ** 0.5)
                  STT(gy, gy, 1.0, py, ALU.add, ALU.mult)
                  ACT(gy, gy, AF.Sigmoid, scale=GC1)
                  hf = f512.next()
                  STT(hf, S4[c][i], hin[:, c:c + 1], S3[c][i], ALU.mult, ALU.add)
                  TT_("dve", hf, hf, gy, ALU.mult)
                  TT_("dve", S3[c][i], hf, py, ALU.mult)
          CK(7)
          for half in range(2):
              GUARD([wreg])
              WO = wview(512)
              wob = [Buf() for _ in range(KC)]
              load_cols(WO, wob, w_out_d[l], 512 * half, 512 * half + 512, 0)
              for i in range(NT):
                  for o4 in range(4):
                      oc = 4 * half + o4
                      po = psr.next()
                      for kk in range(KC):
                          src = S1[kk][i] if kk < 4 else S3[kk - 4][i]
                          MM(po, TV(WO[:, kk, 128 * o4:128 * o4 + 128], wob[kk]), src,
                             start=(kk == 0), stop=(kk == KC - 1), xr=[wreg])
                      TT_("dve", H[oc][i], H[oc][i], po, ALU.add)

          CK(8)
          halo_send(2)
          CK(9)
          for i in range(NT):
              rmsnorm(lambda kc: H[kc][i], lambda kc: spc(SP_LNF + kc), lambda kc: U[kc][i], TT)
          CK(10)
          halo_recv(2, lambda kc: spc(SP_LNF + kc))
          MEMSET(dummy2, 0.0, eng="dve", xw=[wreg])
          SQ3 = GC3 ** 0.5

          def ffn_load(g):
              fin, fdn = WF[g % 2]
              gd = wf_guard[g % 2]
              fa_b = [Buf() for _ in range(KC)]
              fg_b = [Buf() for _ in range(KC)]
              fd_b = [Buf() for _ in range(FG)]
              GUARD([gd] + (store_bufs if g < 2 else []))
              load_cols(fin, fa_b, f_in_d[l], 512 * g, 512 * g + 512, 0)
              load_cols(fin, fg_b, f_in_d[l], FH + 512 * g, FH + 512 * g + 512, 512)
              for j in range(FG):
                  DMA("pool", fdn[:, j, :], f_dn_d[l][512 * g + 128 * j:512 * g + 128 * j + 128, :], [], [fd_b[j]])
              return (fin, fdn, gd, fa_b, fg_b, fd_b)

          def stage_a1(W, g, i, jj):
              fin, fdn, gd, fa_b, fg_b, fd_b = W
              zss = []
              for half in range(2):
                  z = (24 * half) + FG * g + jj
                  wb = fa_b if half == 0 else fg_b
                  col = 512 * half + 128 * jj
                  pz = psf.next()
                  for kc in range(KC):
                      MM(pz, TV(fin[:, kc, col:col + 128], wb[kc]), U[kc][i],
                         start=(kc == 0), stop=(kc == KC - 1), xr=[gd])
                  zs = fzr.next()
                  CP("act", zs[:, 2:514], pz)
                  if i == 0:
                      ph = psf.next()
                      for kc in range(KC):
                          MM(ph[:, 0:2], TV(fin[:, kc, col:col + 128], wb[kc]), TV(uh.ap[:, kc, 0:2], uh.buf),
                             start=(kc == 0), stop=(kc == KC - 1), xr=[gd])
                      CP("act", zs[:, 0:2], ph[:, 0:2])
                  else:
                      CP("pool", zs[:, 0:2], ztail[z])
                  if i < NT - 1:
                      CP("pool", ztail[z], zs[:, 512:514])
                  zss.append((z, zs))
              return zss

          def stage_a2(zss):
              accs = []
              for (z, zs) in zss:
                  acc = ffr.next()
                  fw = lambda j: spc(SP_FCW + 3 * z + j)
                  ACT(acc, zs[:, 2:514], AF.Identity, bias=spc(SP_FCB + z), scale=fw(2))
                  STT(acc, zs[:, 1:513], fw(1), acc, ALU.mult, ALU.add)
                  STT(acc, zs[:, 0:512], fw(0), acc, ALU.mult, ALU.add)
                  accs.append(acc)
              return accs

          def stage_b1(accs):
              x2 = ffr.next()
              ACT(x2, accs[0], AF.Square, scale=SQ3)
              STT(x2, x2, 1.0, accs[0], ALU.add, ALU.mult)
              ACT(x2, x2, AF.Sigmoid, scale=GC1)
              return x2

          def stage_b2(accs, x2):
              TT_("pool", x2, x2, accs[0], ALU.mult)
              ab = b512.next()
              TT_("pool", ab, x2, accs[1], ALU.mult)
              return ab

          def down(W, i, acts):
              fin, fdn, gd, fa_b, fg_b, fd_b = W
              for oc in range(KC):
                  pd = psf.next()
                  for jj in range(FG):
                      MM(pd, TV(fdn[:, jj, 128 * oc:128 * oc + 128], fd_b[jj]), acts[jj],
                         start=(jj == 0), stop=(jj == FG - 1), xr=[gd])
                  TT_("dve", H[oc][i], H[oc][i], pd, ALU.add)

          units = [(g, i, jj) for g in range(NG) for i in range(NT) for jj in range(FG)]
          NU = len(units)
          Wg = {0: ffn_load(0)}
          st = {}
          acts = []
          pend_down = None
          for k in range(NU + 5):
              if pend_down is not None:
                  down(*pend_down)
                  pend_down = None
              if k < NU:
                  g, i, jj = units[k]
                  st[k] = {"W": Wg[g], "i": i, "jj": jj}
                  st[k]["zss"] = stage_a1(Wg[g], g, i, jj)
                  if i == 1 and jj == 1 and g + 1 < NG:
                      Wg[g + 1] = ffn_load(g + 1)
              if 0 <= k - 1 < NU:
                  st[k - 1]["accs"] = stage_a2(st[k - 1]["zss"])
              if 0 <= k - 2 < NU:
                  st[k - 2]["x2"] = stage_b1(st[k - 2]["accs"])
              if 0 <= k - 3 < NU:
                  s3 = st.pop(k - 3)
                  acts.append(stage_b2(s3["accs"], s3["x2"]))
                  if s3["jj"] == FG - 1:
                      pend_down = (s3["W"], s3["i"], acts)
                      acts = []
          assert pend_down is None and not acts
          MEMSET(dummy2, 0.0, eng="dve", xw=[wreg] + ffx_bufs)
          if l + 1 < L:
              halo_send(3)

    except _Stop:
        pass

    finals = []
    if final_norm:
        for i in range(NT):
            ss = psr.next()
            for kc in range(KC):
                sq = b512.next()
                ACT(sq, H[kc][i], AF.Square)
                MM(ss, ones_bf, sq, start=(kc == 0), stop=(kc == KC - 1))
            t = f512.next()
            ACT(t, ss, AF.Ln, bias=epsc, scale=1.0 / D)
            ACT(t, t, AF.Exp, scale=-0.5)
            for kc in range(KC):
                STT(H[kc][i], H[kc][i], cst[:, CS_LNF + kc:CS_LNF + kc + 1], t, ALU.mult, ALU.mult)
                finals.append(DMA("sp", hout_d[:, kc, tsl(i)], H[kc][i].ap, [H[kc][i]], []))
    else:
        for i in range(NT):
            for kc in range(KC):
                finals.append(DMA("sp", hout_d[:, kc, tsl(i)], H[kc][i].ap, [H[kc][i]], []))
    kb.emit(final_ops=finals)
    return nc, es


def _fm(a):
    t = a.shape[0]
    return np.ascontiguousarray(a.reshape(t, KC, 128).transpose(2, 1, 0))


def _pack_params(inp, layers):
    L = len(layers)
    spar = np.zeros((L, 128, NSP), np.float32)
    lruw = np.zeros((L, 128, 1024), np.float32)
    w2a = np.zeros((L, 32, 256), np.float32)
    for li, l in enumerate(layers):
        spar[li, :, SP_LNM:SP_LNM + 8] = inp["ln_mix"][l].reshape(8, 128).T
        spar[li, :, SP_LNF:SP_LNF + 8] = inp["ln_ffn"][l].reshape(8, 128).T
        spar[li, :, SP_GN] = inp["gla_norm"][l]
        cw = inp["lru_conv_w"][l].reshape(4, 4, 128)
        spar[li, :, SP_LCW:SP_LCW + 16] = cw.transpose(2, 1, 0).reshape(128, 16)
        spar[li, :, SP_LCB:SP_LCB + 4] = inp["lru_conv_b"][l].reshape(4, 128).T
        spar[li, :, SP_BA:SP_BA + 4] = inp["lru_ba"][l].reshape(4, 128).T
        spar[li, :, SP_BX:SP_BX + 4] = inp["lru_bx"][l].reshape(4, 128).T
        spar[li, :, SP_LAM:SP_LAM + 4] = inp["lru_lambda"][l].reshape(4, 128).T
        fw = inp["ffn_conv_w"][l].reshape(3, 48, 128)
        spar[li, :, SP_FCW:SP_FCW + 144] = fw.transpose(2, 1, 0).reshape(128, 144)
        spar[li, :, SP_FCB:SP_FCB + 48] = inp["ffn_conv_b"][l].reshape(48, 128).T
        for c in range(4):
            for gi, key in enumerate(("lru_wa", "lru_wx")):
                blk = np.zeros((128, 128), np.float32)
                blk[0:64, 0:64] = inp[key][l][2 * c]
                blk[64:128, 64:128] = inp[key][l][2 * c + 1]
                lruw[li, :, (2 * c + gi) * 128:(2 * c + gi + 1) * 128] = blk
        w2a[li, 0:16, :] = inp["gla_gate_w2"][l]
        w2a[li, 16, :] = inp["gla_gate_b"][l]
    return spar, lruw, w2a


def _consts(c, ln_final):
    cst = np.zeros((128, NCST), np.float32)
    s = np.arange(128)[:, None]
    t = np.arange(128)[None, :]
    cst[:, CS_TRI:CS_TRI + 128] = (s > t)
    m = (s <= t).astype(np.float32)
    cst[:, CS_MASK:CS_MASK + 128] = m
    cst[:, CS_MASK + 128:CS_MASK + 256] = m
    cst[:, CS_MSK:CS_MSK + 8] = (np.arange(8) < c)[None, :]
    cst[:, CS_SEL:CS_SEL + 8] = (np.arange(8) == c - 1)[None, :]
    cst[:, CS_FF] = 1.0 if c == 0 else 0.0
    cst[:, CS_FF + 1] = 0.0 if c == 0 else 1.0
    cst[:, CS_LNF:CS_LNF + 8] = ln_final.reshape(8, 128).T
    return cst


_PROG = {}


def _get_prog(L, final_norm):
    key = (L, final_norm)
    if key not in _PROG:
        _PROG[key] = build_program(L, final_norm)
    return _PROG[key][0]


def _run(h_fm_list, halo_list, inp, layers, final_norm):
    L = len(layers)
    nc = _get_prog(L, final_norm)
    spar, lruw, w2a = _pack_params(inp, layers)
    sl = slice(layers[0], layers[-1] + 1)
    w_in = np.ascontiguousarray(inp["w_in"][sl])
    w_out = np.ascontiguousarray(inp["w_out"][sl])
    f_in = np.ascontiguousarray(inp["ffn_w_in"][sl])
    f_dn = np.ascontiguousarray(inp["ffn_w_down"][sl])
    in_maps = []
    for c in range(NCORES):
        in_maps.append({
            "hin": h_fm_list[c], "halo": halo_list[c], "w_in": w_in, "w_out": w_out, "f_in": f_in, "f_dn": f_dn,
            "spar": spar, "lruw": lruw, "w2a": w2a, "cst": _consts(c, inp["ln_final"]),
        })
    res = run_bass_kernel_spmd(nc, in_maps, core_ids=list(range(NCORES)))
    return [r["hout"] for r in res.results]


def _halos(h_fm_list):
    out = []
    for c in range(NCORES):
        if c == 0:
            out.append(np.zeros((128, KC * 3), np.float32))
        else:
            out.append(np.ascontiguousarray(h_fm_list[c - 1][:, :, T - 3:T].reshape(128, KC * 3)))
    return out


FUSED = True


def kernel(**inputs):
    inp = {k: np.asarray(v, dtype=np.float32) for k, v in inputs.items()}
    x = inp["x"][0]
    h = [_fm(x[c * T:(c + 1) * T]) for c in range(NCORES)]
    if FUSED:
        h = _run(h, _halos(h), inp, list(range(DEPTH)), True)
    else:
        for l in range(DEPTH):
            h = _run(h, _halos(h), inp, [l], l == DEPTH - 1)
    out = np.concatenate([o.transpose(2, 1, 0).reshape(T, D) for o in h], axis=0)
    return out[None].astype(np.float32)
```

```python
import contextlib
from functools import partial

import numpy as np
import concourse.bass as bass
import concourse.mybir as mybir
from concourse.bass_utils import run_bass_kernel_spmd

F32 = mybir.dt.float32
BF16 = mybir.dt.bfloat16
ALU = mybir.AluOpType
AF = mybir.ActivationFunctionType

NCORES = 8
SEQ = 16384
D = 1024
T = SEQ // NCORES
TT = 512
NT = T // TT
KC = 8
DEPTH = 4
D_IN = 2576
QC, KCOL, VC, GC, AC, XC, YC = 0, 256, 512, 1024, 1536, 1552, 2064
FH = 3072
EPS = 1e-6
FG = 4
NG = 24 // FG

SP_LNM, SP_LNF, SP_GN, SP_LCW, SP_LCB, SP_BA, SP_BX, SP_LAM, SP_FCW, SP_FCB = (
    0, 8, 16, 17, 33, 37, 41, 45, 49, 193)
NSP = 241
CS_TRI, CS_MASK, CS_MSK, CS_SEL, CS_FF, CS_LNF = 0, 128, 384, 392, 400, 402
NCST = 410
PAYW = 528
GC1 = 1.5957691216057308
GC3 = 0.044715


class Buf:
    __slots__ = ("name", "w", "r")

    def __init__(self, name=""):
        self.name = name
        self.w = None
        self.r = []


class Op:
    __slots__ = ("eng", "fn", "deps", "signal", "event", "kind")

    def __init__(self, eng, fn, kind):
        self.eng = eng
        self.fn = fn
        self.kind = kind
        self.deps = []
        self.signal = False
        self.event = None


class KB:
    EPOCH = 30000

    def __init__(self, nc, es):
        self.nc = nc
        self.es = es
        self.ops = []
        self.engs = {"pe": nc.tensor, "act": nc.scalar, "dve": nc.vector,
                     "pool": nc.gpsimd, "sp": nc.sync}

    def sem(self, name):
        return self.es.enter_context(self.nc.semaphore(name))

    def _add(self, op, reads, writes):
        deps = []
        for b in reads:
            if b.w is not None:
                deps.append(b.w)
        for b in writes:
            if b.w is not None:
                deps.append(b.w)
            deps.extend(b.r)
        seen = set()
        for d in deps:
            if d is op or id(d) in seen:
                continue
            if d.eng == "pe" and op.eng == "pe" and d.kind == "c" and op.kind == "c":
                continue
            seen.add(id(d))
            op.deps.append(d)
            d.signal = True
        for b in writes:
            b.w = op
            b.r = []
        for b in reads:
            if b.w is not op:
                b.r.append(op)
        self.ops.append(op)
        return op

    def op(self, eng, fn, reads=(), writes=()):
        return self._add(Op(eng, fn, "c"), list(reads), list(writes))

    def dma(self, eng, fn, reads=(), writes=()):
        o = self._add(Op(eng, fn, "dma"), list(reads), list(writes))
        o.signal = True
        return o

    def cc(self, fn, reads=(), writes=()):
        o = self._add(Op("pool", fn, "cc"), list(reads), list(writes))
        o.signal = True
        return o

    def emit(self, final_ops=()):
        nc = self.nc
        ndma = 12
        esem = {e: [] for e in self.engs}
        ecnt = {e: 0 for e in self.engs}
        dsem = {e: [(self.sem(f"d_{e}_{i}"), [0]) for i in range(ndma)] for e in ("pool", "sp")}
        drr = {e: 0 for e in dsem}
        ccsem = self.sem("cc")
        cccnt = 0
        waited = {e: {} for e in self.engs}

        def wait_on(eng, evs, ins_fn, embed=True):
            need = []
            for s, (sh, v) in evs.items():
                if waited[eng].get(s, 0) < v:
                    need.append((sh, v))
                    waited[eng][s] = v
            for (sh, v) in (need[1:] if embed else need):
                self.engs[eng].wait_ge(sh, v)
            ins = ins_fn()
            if need and embed:
                ins._wait_ge(need[0][0], need[0][1])
            return ins

        for op in self.ops:
            evs = {}
            for d in op.deps:
                sh, v = d.event
                k = id(sh)
                if k not in evs or evs[k][1] < v:
                    evs[k] = (sh, v)
            eng = op.eng
            if op.kind == "c":
                ins = wait_on(eng, evs, op.fn)
                if op.signal:
                    if ecnt[eng] % self.EPOCH == 0:
                        esem[eng].append(self.sem(f"e_{eng}_{len(esem[eng])}"))
                    ecnt[eng] += 1
                    v = (ecnt[eng] - 1) % self.EPOCH + 1
                    ins.then_inc(esem[eng][-1], 1)
                    op.event = (esem[eng][-1], v)
            elif op.kind == "dma":
                i = drr[eng] % ndma
                drr[eng] += 1
                sh, cnt = dsem[eng][i]
                if cnt[0] > 0:
                    k = id(sh)
                    if k not in evs or evs[k][1] < cnt[0] * 16:
                        evs[k] = (sh, cnt[0] * 16)
                ins = wait_on(eng, evs, op.fn)
                cnt[0] += 1
                ins.then_inc(sh, 16)
                op.event = (sh, cnt[0] * 16)
            else:
                ins = wait_on(eng, evs, op.fn, embed=False)
                cccnt += 1
                csem = self.sem(f"cc{cccnt}")
                ins.then_inc(csem)
                op.event = (csem, 1)
        evs = {}
        for d in final_ops:
            sh, v = d.event
            k = id(sh)
            if k not in evs or evs[k][1] < v:
                evs[k] = (sh, v)
        for k, (sh, v) in evs.items():
            nc.sync.wait_ge(sh, v)


class TV:
    __slots__ = ("ap", "buf")

    def __init__(self, ap, buf=None):
        self.ap = ap
        self.buf = buf if buf is not None else Buf()

    def __getitem__(self, idx):
        return TV(self.ap[idx], self.buf)


class Ring:
    def __init__(self, tvs):
        self.tvs = tvs
        self.i = 0

    def next(self):
        t = self.tvs[self.i % len(self.tvs)]
        self.i += 1
        return t


class _Stop(Exception):
    pass


def build_program(L, final_norm, stop=99):
    nc = bass.Bass("TRN2", target_bir_lowering=False)
    es = contextlib.ExitStack()
    kb = KB(nc, es)

    dt = nc.dram_tensor
    hin_d = dt("hin", [128, KC, T], F32, kind="ExternalInput").ap()
    halo_d = dt("halo", [128, KC * 3], F32, kind="ExternalInput").ap()
    w_in_d = dt("w_in", [L, D, D_IN], F32, kind="ExternalInput").ap()
    w_out_d = dt("w_out", [L, D, D], F32, kind="ExternalInput").ap()
    f_in_d = dt("f_in", [L, D, 2 * FH], F32, kind="ExternalInput").ap()
    f_dn_d = dt("f_dn", [L, FH, D], F32, kind="ExternalInput").ap()
    spar_d = dt("spar", [L, 128, NSP], F32, kind="ExternalInput").ap()
    lruw_d = dt("lruw", [L, 128, 1024], F32, kind="ExternalInput").ap()
    w2a_d = dt("w2a", [L, 32, 256], F32, kind="ExternalInput").ap()
    cst_d = dt("cst", [128, NCST], F32, kind="ExternalInput").ap()
    hout_d = dt("hout", [128, KC, T], F32, kind="ExternalOutput").ap()
    ib1 = dt("ib1", [128, PAYW], F32)
    ob1 = dt("ob1", [128 * NCORES, PAYW], F32)
    ib2 = dt("ib2", [128, 24], F32)
    ob2 = dt("ob2", [128 * NCORES, 24], F32)
    ib1_b, ob1_b, ib2_b, ob2_b = Buf("ib1"), Buf("ob1"), Buf("ib2"), Buf("ob2")

    cap = (nc.sbuf_bytes_remaining // 64) * 64 - 64
    arena = nc.alloc_sbuf_tensor("arena", [128, cap // 2], BF16)
    pos = [0]

    def alloc(nbytes):
        o = pos[0]
        pos[0] += (nbytes + 63) // 64 * 64
        assert pos[0] <= cap, f"SBUF overflow {pos[0]} > {cap}"
        return o

    def view(off, nbytes, dtype, parts=128):
        ap = arena[0:parts, off // 2:(off + nbytes) // 2]
        return ap.bitcast(F32) if dtype == F32 else ap

    def mk(shape, dtype, off=None, parts=128):
        n = int(np.prod(shape[1:])) * (4 if dtype == F32 else 2)
        if off is None:
            off = alloc(n)
        ap = view(off, n, dtype, parts)
        if len(shape) == 3:
            ap = ap.rearrange("p (a b) -> p a b", a=shape[1])
        return ap

    H_ap = mk([128, KC, T], F32)
    U_ap = mk([128, KC, T], BF16)
    s1_off = pos[0]
    S1_ap = mk([128, 4, T], BF16)
    S2_ap = mk([128, 2, T], BF16)
    S3_ap = mk([128, 4, T], BF16)
    S4_ap = mk([128, 4, T], BF16)
    tsl = lambda i: slice(i * TT, (i + 1) * TT)
    H = [[TV(H_ap[:, kc, tsl(i)]) for i in range(NT)] for kc in range(KC)]
    U = [[TV(U_ap[:, kc, tsl(i)]) for i in range(NT)] for kc in range(KC)]
    S1 = [[TV(S1_ap[:, h, tsl(i)]) for i in range(NT)] for h in range(4)]
    S2 = [[TV(S2_ap[:, p, tsl(i)]) for i in range(NT)] for p in range(2)]
    S3 = [[TV(S3_ap[:, c, tsl(i)]) for i in range(NT)] for c in range(4)]
    S4 = [[TV(S4_ap[:, c, tsl(i)]) for i in range(NT)] for c in range(4)]
    WF = []
    for par in range(2):
        base = s1_off + par * 24576
        fin = view(base, 16384, BF16).rearrange("p (a b) -> p a b", a=KC)
        fdn = view(base + 16384, 8192, BF16).rearrange("p (a b) -> p a b", a=FG)
        WF.append((fin, fdn))
    store_bufs = ([t.buf for r in S1 for t in r] + [t.buf for r in S2 for t in r]
                  + [t.buf for r in S3 for t in r] + [t.buf for r in S4 for t in r])
    wf_guard = [Buf("wf0"), Buf("wf1")]

    sp = TV(mk([128, NSP], F32))
    cst = TV(mk([128, NCST], F32))
    ones = TV(mk([128, 512], BF16))
    ones_bf = ones[:, 0:128]
    glr = TV(mk([32, 512], F32, parts=32))
    w2a = TV(mk([32, 256], F32, parts=32))
    lruw = TV(mk([128, 1024], F32))
    Sf = [TV(mk([128, 256], F32)) for _ in range(2)]
    Sbf = [TV(mk([128, 256], BF16)) for _ in range(2)]
    smalls = mk([128, 256], F32)
    _sp = [0]

    def small(n):
        o = _sp[0]
        _sp[0] += n
        assert _sp[0] <= 256
        return TV(smalls[:, o:o + n])

    offs = [small(8) for _ in range(2)]
    eoff = [small(4) for _ in range(2)]
    hst, Pst, hin = small(4), small(4), small(4)
    clc = small(8)
    xtail = [small(4) for _ in range(4)]
    tl = small(24)
    sm = [small(8) for _ in range(6)]
    epsc = small(1)
    dummy = small(1)
    dummy2 = small(1)
    zt_ap = mk([128, 96], F32)
    ztail = [TV(zt_ap[:, 2 * z:2 * z + 2]) for z in range(48)]
    uh = TV(mk([128, KC, 4], BF16))
    kz = [TV(mk([128, 512], BF16)) for _ in range(2)]

    w_base = pos[0]
    n_f512, n_b512, n_f256, n_b256, n_f514 = 6, 5, 2, 2, 3
    work_bytes = n_f512 * 2048 + n_b512 * 1024 + n_f256 * 1024 + n_b256 * 512 + n_f514 * 2112
    work_off = cap - work_bytes - 64
    wreg_bytes = work_off - w_base
    assert wreg_bytes >= 12 * 1024, f"weights area too small: {wreg_bytes}"
    wpos = [work_off]

    def walloc(shape, dtype):
        n = int(np.prod(shape[1:])) * (4 if dtype == F32 else 2)
        o = wpos[0]
        wpos[0] += (n + 63) // 64 * 64
        assert wpos[0] <= cap
        return view(o, n, dtype)

    f512 = Ring([TV(walloc([128, 512], F32)) for _ in range(n_f512)])
    b512 = Ring([TV(walloc([128, 512], BF16)) for _ in range(n_b512)])
    f256 = Ring([TV(walloc([128, 256], F32)) for _ in range(n_f256)])
    b256 = Ring([TV(walloc([128, 256], BF16)) for _ in range(n_b256)])
    f514 = Ring([TV(walloc([128, 528], F32)) for _ in range(n_f514)])
    wreg = Buf("wreg")
    n_ffx = 4
    ffx = [TV(view(w_base + 2048 * k, 2048, F32)) for k in range(n_ffx)]
    n_fzx = min(2, (wreg_bytes - 2048 * n_ffx) // 2112)
    fzx = [TV(view(w_base + 2048 * n_ffx + 2112 * k, 2112, F32)) for k in range(n_fzx)]
    ffx_bufs = [t.buf for t in ffx] + [t.buf for t in fzx]
    ffr = Ring(f512.tvs + ffx)
    fzr = Ring(f514.tvs + fzx)

    PS = [TV(nc.alloc_psum_tensor(f"ps{i}", [128, 512], F32)[:, :]) for i in range(8)]
    PO = PS[0:2]
    psr = Ring(PS[2:8])
    psf = Ring(PS[0:8])

    def B(xs):
        out = []
        for x in xs:
            if x is None or isinstance(x, (int, float)):
                continue
            if isinstance(x, TV):
                out.append(x.buf)
            elif isinstance(x, Buf):
                out.append(x)
            else:
                out.extend(B(x))
        return out

    def A(x):
        return x.ap if isinstance(x, TV) else x

    def MM(out, lhsT, rhs, start=True, stop=True, xr=()):
        kb.op("pe", partial(nc.tensor.matmul, out.ap, lhsT.ap, rhs.ap, start=start, stop=stop),
              B([lhsT, rhs, xr]), B([out]))

    def ACT(out, in_, func, bias=0.0, scale=1.0, xr=()):
        kb.op("act", partial(nc.scalar.activation, out.ap, in_.ap, func, bias=A(bias), scale=A(scale)),
              B([in_, bias, scale, xr]), B([out]))

    def TT_(eng, out, in0, in1, op, xr=()):
        e = nc.vector if eng == "dve" else nc.gpsimd
        kb.op(eng, partial(e.tensor_tensor, out.ap, in0.ap, in1.ap, op), B([in0, in1, xr]), B([out]))

    def TS(eng, out, in0, s1, s2, op0, op1=None, xr=()):
        e = nc.vector if eng == "dve" else nc.gpsimd
        if op1 is None:
            fn = partial(e.tensor_scalar, out.ap, in0.ap, A(s1), None, op0)
        else:
            fn = partial(e.tensor_scalar, out.ap, in0.ap, A(s1), A(s2), op0, op1)
        kb.op(eng, fn, B([in0, s1, s2, xr]), B([out]))

    def STT(out, in0, s, in1, op0, op1, xr=()):
        kb.op("dve", partial(nc.vector.scalar_tensor_tensor, out.ap, in0.ap, A(s), in1.ap, op0, op1),
              B([in0, s, in1, xr]), B([out]))

    def SCAN(out, d0, d1, init, op0, op1, xr=()):
        kb.op("dve", partial(nc.vector.tensor_tensor_scan, out.ap, d0.ap, d1.ap, A(init), op0, op1),
              B([d0, d1, init, xr]), B([out]))

    def CP(eng, out, in_, xr=()):
        if eng == "act":
            kb.op("act", partial(nc.scalar.copy, out.ap, in_.ap), B([in_, xr]), B([out]))
        else:
            e = nc.vector if eng == "dve" else nc.gpsimd
            kb.op(eng, partial(e.tensor_copy, out.ap, in_.ap), B([in_, xr]), B([out]))

    def MEMSET(out, val, eng="dve", xw=()):
        e = nc.vector if eng == "dve" else nc.gpsimd
        kb.op(eng, partial(e.memset, out.ap, val), [], B([out, xw]))

    def DMA(q, out_ap, in_ap, reads, writes):
        e = nc.gpsimd if q == "pool" else nc.sync
        return kb.dma(q, partial(e.dma_start, out=out_ap, in_=in_ap), B(reads), B(writes))

    def GUARD(bufs):
        MEMSET(dummy, 0.0, eng="pool", xw=bufs)

    def spc(col, n=1):
        return sp[:, col:col + n]

    def load_cols(dst3, bufs_k, src2d, c0, c1, dc0):
        n = c1 - c0
        for k in range(len(bufs_k)):
            DMA("pool", dst3[:, k, dc0:dc0 + n], src2d[k * 128:(k + 1) * 128, c0:c1], [], [bufs_k[k]])

    def wview(ncols, nk=KC, off=0):
        ap = view(w_base + off, nk * ncols * 2, BF16).rearrange("p (a b) -> p a b", a=nk)
        assert off + nk * ncols * 2 <= wreg_bytes, (off, nk, ncols, wreg_bytes)
        return ap

    def rmsnorm(src, gcol, dst, n, nfeat=D):
        ss = psr.next()
        for kc in range(KC):
            sq = b512.next()
            ACT(sq[:, 0:n], src(kc), AF.Square)
            MM(ss[:, 0:n], ones_bf, sq[:, 0:n], start=(kc == 0), stop=(kc == KC - 1))
        t = f512.next()
        ACT(t[:, 0:n], ss[:, 0:n], AF.Ln, bias=epsc, scale=1.0 / nfeat)
        ACT(t[:, 0:n], t[:, 0:n], AF.Exp, scale=-0.5)
        for kc in range(KC):
            STT(dst(kc), src(kc), gcol(kc), t[:, 0:n], ALU.mult, ALU.mult)

    def allgather(ib, ib_b, ob, ob_b):
        kb.cc(partial(nc.gpsimd.collective_compute, "AllGather", ALU.bypass,
                      replica_groups=[list(range(NCORES))], ins=[ib.ap().opt()], outs=[ob.ap().opt()]),
              [ib_b], [ob_b])

    def halo_exchange(ntok, gcol, first_src=None):
        if first_src is None:
            pay = f514.next()
            MEMSET(pay[:, 0:24], 0.0)
            for kc in range(KC):
                CP("act", pay[:, kc * ntok:(kc + 1) * ntok], H[kc][NT - 1][:, TT - ntok:TT])
            DMA("pool", ib2.ap(), pay.ap[:, 0:24], [pay], [ib2_b])
            allgather(ib2, ib2_b, ob2, ob2_b)
            g2 = f514.next()
            g2v = g2.ap[:, 0:NCORES * 24].rearrange("p (c n) -> p c n", c=NCORES)
            DMA("pool", g2v, ob2.ap().rearrange("(c p) n -> p c n", p=128), [ob2_b], [g2])
            MEMSET(tl, 0.0)
            for j in range(NCORES):
                STT(tl[:, 0:KC * ntok], TV(g2v[:, j, 0:KC * ntok], g2.buf), cst[:, CS_SEL + j:CS_SEL + j + 1],
                    tl[:, 0:KC * ntok], ALU.mult, ALU.add)
        else:
            DMA("sp", tl.ap[:, 0:KC * ntok], first_src[:, 0:KC * ntok], [], [tl])
        rmsnorm(lambda kc: tl[:, kc * ntok:(kc + 1) * ntok], gcol,
                lambda kc: TV(uh.ap[:, kc, 0:ntok], uh.buf), ntok)

    MEMSET(epsc, EPS)
    DMA("sp", cst.ap, cst_d, [], [cst])
    for kc in range(KC):
        for i in range(NT):
            DMA("sp", H[kc][i].ap, hin_d[:, kc, tsl(i)], [], [H[kc][i]])
    MEMSET(ones, 1.0)
    MEMSET(glr, 1.0)
    MEMSET(kz[0], 0.0)
    MEMSET(kz[1], 0.0)
    trigt = cst[:, CS_TRI:CS_TRI + 128]
    mask2 = cst[:, CS_MASK:CS_MASK + 256]
    msk = cst[:, CS_MSK:CS_MSK + 8]

    def CK(k):
        if stop == k:
            raise _Stop()

    try:
      for l in range(L):
          DMA("sp", sp.ap, spar_d[l], [], [sp])
          DMA("sp", lruw.ap, lruw_d[l], [], [lruw])
          DMA("sp", w2a.ap, w2a_d[l], [], [w2a])
          ACT(sm[0][:, 0:4], spc(SP_LAM, 4), AF.Exp, scale=-1.0)
          ACT(sm[0][:, 0:4], sm[0][:, 0:4], AF.Ln, bias=1.0)
          TS("dve", clc[:, 0:4], sm[0][:, 0:4], -8.0, None, ALU.mult)
          TS("dve", clc[:, 4:8], sm[0][:, 0:4], -16.0, None, ALU.mult)

          halo_exchange(3, lambda kc: spc(SP_LNM + kc), first_src=(halo_d if l == 0 else None))

          CK(1)
          for i in range(NT):
              rmsnorm(lambda kc: H[kc][i], lambda kc: spc(SP_LNM + kc), lambda kc: U[kc][i], TT)

          CK(2)
          for p in range(2):
              GUARD([wreg])
              WA = wview(528)
              wab = [Buf() for _ in range(KC)]
              wib = w_in_d[l]
              load_cols(WA, wab, wib, QC + 128 * p, QC + 128 * p + 128, 0)
              wkb = [Buf() for _ in range(KC)]
              load_cols(WA, wkb, wib, KCOL + 128 * p, KCOL + 128 * p + 128, 128)
              wvb = [Buf() for _ in range(KC)]
              load_cols(WA, wvb, wib, VC + 256 * p, VC + 256 * p + 256, 256)
              wgb = [Buf() for _ in range(KC)]
              load_cols(WA, wgb, wib, AC, AC + 16, 512)
              voff = KC * 528 * 2
              v_ap = view(w_base + voff, 4 * 256 * 2, BF16).rearrange("p (a b) -> p a b", a=4)
              kd_ap = view(w_base + voff + 2048, 4 * 128 * 2, BF16).rearrange("p (a b) -> p a b", a=4)
              assert voff + 3072 <= wreg_bytes
              vS = [TV(v_ap[:, s, :]) for s in range(4)]
              kdS = [TV(kd_ap[:, s, :]) for s in range(4)]
              MEMSET(Sf[p], 0.0)
              MEMSET(Sbf[p], 0.0)
              MEMSET(offs[p], 0.0)
              for i in range(NT):
                  pg = psr.next()
                  for kc in range(KC):
                      MM(pg[0:16, :], TV(WA[:, kc, 512:528], wgb[kc]), U[kc][i], start=(kc == 0), stop=(kc == KC - 1),
                         xr=[wreg])
                  CP("act", glr[0:16, :], pg[0:16, :])
                  CK(2.1)
                  for s in range(4):
                      ssl = slice(s * 128, (s + 1) * 128)
                      pxg = psr.next()
                      MM(pxg[:, 0:128], glr[0:17, ssl], w2a[0:17, 128 * p:128 * p + 128])
                      e2 = f256.next()
                      ACT(e2[:, 0:128], pxg[:, 0:128], AF.Exp, scale=-1.0)
                      ACT(e2[:, 0:128], e2[:, 0:128], AF.Ln, bias=1.0)
                      pc = psr.next()
                      MM(pc[:, 0:128], trigt, e2[:, 0:128])
                      edec = f256.next()
                      ACT(edec[:, 0:128], pc[:, 0:128], AF.Exp, scale=-1.0 / 16.0)
                      pkt = psr.next()
                      for kc in range(KC):
                          MM(pkt[:, 0:128], U[kc][i][:, ssl], TV(WA[:, kc, 128:256], wkb[kc]),
                             start=(kc == 0), stop=(kc == KC - 1), xr=[wreg])
                      TT_("dve", kdS[s], pkt[:, 0:128], edec[:, 0:128], ALU.mult, xr=[wreg])
                      pv = psr.next()
                      for kc in range(KC):
                          MM(pv[:, 0:256], U[kc][i][:, ssl], TV(WA[:, kc, 256:512], wvb[kc]),
                             start=(kc == 0), stop=(kc == KC - 1), xr=[wreg])
                      CP("act", vS[s], pv[:, 0:256], xr=[wreg])
                  CK(2.2)
                  pxt = psr.next()
                  MM(pxt, w2a[0:17, 128 * p:128 * p + 128], glr[0:17, :])
                  e1 = f512.next()
                  ACT(e1, pxt, AF.Exp, scale=-1.0)
                  spf = f512.next()
                  ACT(spf, e1, AF.Ln, bias=1.0)
                  cs = f512.next()
                  for s in range(4):
                      ssl = slice(s * 128, (s + 1) * 128)
                      SCAN(cs[:, ssl], ones[:, 0:128], spf[:, ssl], 0.0, ALU.mult, ALU.add)
                  eq = f512.next()
                  ACT(eq, cs, AF.Exp, scale=-1.0 / 16.0)
                  ek = e1
                  ACT(ek, cs, AF.Exp, scale=1.0 / 16.0)
                  for s in range(4):
                      TT_("dve", offs[p][:, s + 1:s + 2], offs[p][:, s:s + 1], cs[:, 128 * s + 127:128 * s + 128], ALU.add)
                  ACT(eoff[p], offs[p][:, 0:4], AF.Exp, scale=-1.0 / 16.0)
                  CK(2.3)
                  pq = psr.next()
                  for kc in range(KC):
                      MM(pq, TV(WA[:, kc, 0:128], wab[kc]), U[kc][i], start=(kc == 0), stop=(kc == KC - 1), xr=[wreg])
                  qi = b512.next()
                  STT(qi, pq, 0.125, eq, ALU.mult, ALU.mult)
                  pk = psr.next()
                  for kc in range(KC):
                      MM(pk, TV(WA[:, kc, 128:256], wkb[kc]), U[kc][i], start=(kc == 0), stop=(kc == KC - 1), xr=[wreg])
                  TT_("dve", kz[0][0:64, :], pk[0:64, :], ek[0:64, :], ALU.mult)
                  TT_("dve", kz[1][64:128, :], pk[64:128, :], ek[64:128, :], ALU.mult)
                  for s in range(4):
                      ssl = slice(s * 128, (s + 1) * 128)
                      TS("dve", S2[p][i][:, ssl], qi[:, ssl], eoff[p][:, s:s + 1], None, ALU.mult)
                  CP("dve", offs[p][:, 0:1], offs[p][:, 4:5])
                  CK(2.4)
                  for s in range(4):
                      ssl = slice(s * 128, (s + 1) * 128)
                      pA = psr.next()
                      for a in range(2):
                          MM(pA[:, 128 * a:128 * a + 128], kz[a][:, ssl], qi[:, ssl])
                      At = b256.next()
                      TT_("dve", At, pA[:, 0:256], mask2, ALU.mult)
                      for a in range(2):
                          MM(PO[a][:, ssl], vS[s][:, 128 * a:128 * a + 128], At[:, 128 * a:128 * a + 128],
                             start=True, stop=False, xr=[wreg])
                          MM(PO[a][:, ssl], Sbf[p][:, 128 * a:128 * a + 128], qi[:, ssl], start=False, stop=True)
                      pU = psr.next()
                      MM(pU[:, 0:256], kdS[s], vS[s], xr=[wreg])
                      STT(Sf[p], Sf[p], eq[:, 128 * s + 127:128 * s + 128], pU[:, 0:256], ALU.mult, ALU.add)
                      CP("act", Sbf[p][0:64, 0:128], Sf[p][0:64, 0:128])
                      CP("act", Sbf[p][64:128, 128:256], Sf[p][64:128, 128:256])
                  CK(2.5)
                  for a in range(2):
                      CP("act", S1[2 * p + a][i], PO[a])

          CK(3)
          GUARD([wreg])
          WX = wview(512)
          wxb = [Buf() for _ in range(KC)]
          load_cols(WX, wxb, w_in_d[l], XC, XC + 512, 0)
          for c in range(4):
              px3 = psr.next()
              for kc in range(KC):
                  MM(px3[:, 0:3], TV(WX[:, kc, 128 * c:128 * c + 128], wxb[kc]), TV(uh.ap[:, kc, 0:3], uh.buf),
                     start=(kc == 0), stop=(kc == KC - 1), xr=[wreg])
              CP("act", xtail[c][:, 0:3], px3[:, 0:3])
          for i in range(NT):
              for c in range(4):
                  px = psr.next()
                  for kc in range(KC):
                      MM(px, TV(WX[:, kc, 128 * c:128 * c + 128], wxb[kc]), U[kc][i],
                         start=(kc == 0), stop=(kc == KC - 1), xr=[wreg])
                  xs = f514.next()
                  CP("act", xs[:, 3:515], px)
                  CP("pool", xs[:, 0:3], xtail[c][:, 0:3])
                  CP("pool", xtail[c][:, 0:3], xs[:, 512:515])
                  xc = f512.next()
                  cw = lambda j: spc(SP_LCW + 4 * c + j)
                  ACT(xc, xs[:, 3:515], AF.Identity, bias=spc(SP_LCB + c), scale=cw(3))
                  for j in (2, 1, 0):
                      STT(xc, xs[:, j:j + 512], cw(j), xc, ALU.mult, ALU.add)
                  pa = psr.next()
                  MM(pa, lruw[:, (2 * c) * 128:(2 * c + 1) * 128], xc)
                  pxg = psr.next()
                  MM(pxg, lruw[:, (2 * c + 1) * 128:(2 * c + 2) * 128], xc)
                  r = f512.next()
                  ACT(r, pa, AF.Sigmoid, bias=spc(SP_BA + c))
                  ig = f512.next()
                  ACT(ig, pxg, AF.Sigmoid, bias=spc(SP_BX + c))
                  av = f512.next()
                  ACT(av, r, AF.Exp, scale=clc[:, c:c + 1])
                  ACT(r, r, AF.Exp, scale=clc[:, 4 + c:5 + c])
                  ACT(r, r, AF.Sqrt, bias=1.0, scale=-1.0)
                  if i == 0:
                      TS("dve", r[:, 0:1], r[:, 0:1], cst[:, CS_FF + 1:CS_FF + 2], cst[:, CS_FF:CS_FF + 1],
                         ALU.mult, ALU.add)
                  TT_("dve", ig, ig, xc, ALU.mult)
                  TT_("dve", ig, ig, r, ALU.mult)
                  hl = f512.next()
                  SCAN(hl, av, ig, 0.0 if i == 0 else hst[:, c:c + 1], ALU.mult, ALU.add)
                  Pp = f512.next()
                  SCAN(Pp, av, ones, 1.0 if i == 0 else Pst[:, c:c + 1], ALU.mult, ALU.mult)
                  CP("dve", hst[:, c:c + 1], hl[:, 511:512])
                  CP("dve", Pst[:, c:c + 1], Pp[:, 511:512])
                  CP("pool", S3[c][i], hl)
                  CP("pool", S4[c][i], Pp)

          CK(4)
          pay = f514.next()
          MEMSET(pay[:, 512:528], 0.0)
          for p in range(2):
              CP("act", pay[:, 256 * p:256 * p + 256], Sf[p])
              CP("act", pay[:, 512 + p:513 + p], offs[p][:, 0:1])
          CP("act", pay[:, 514:518], hst)
          CP("act", pay[:, 518:522], Pst)
          DMA("pool", ib1.ap(), pay.ap, [pay], [ib1_b])
          CK(4.1)
          allgather(ib1, ib1_b, ob1, ob1_b)
          CK(4.2)
          ob1v = ob1.ap().rearrange("(c p) n -> p c n", p=128)
          gs = f514.next()
          gsv = gs.ap[:, 0:NCORES * 16].rearrange("p (c n) -> p c n", c=NCORES)
          DMA("pool", gsv, ob1v[:, :, 512:528], [ob1_b], [gs])
          GS = lambda j, a, b: TV(gsv[:, j, a:b], gs.buf)
          CK(4.3)
          for p in range(2):
              GUARD([wreg])
              g1_ap = view(w_base, NCORES * 256 * 4, F32).rearrange("p (c n) -> p c n", c=NCORES)
              assert NCORES * 1024 <= wreg_bytes
              g1b = Buf()
              kb.dma("pool", partial(nc.gpsimd.dma_start, out=g1_ap, in_=ob1v[:, :, 256 * p:256 * p + 256]),
                     [ob1_b, wreg], [g1b])
              csm = sm[1]
              TT_("dve", csm, TV(gsv[:, :, p], gs.buf), msk, ALU.mult)
              dm = sm[2 + p]
              ACT(dm, csm, AF.Exp, scale=-1.0 / 16.0)
              MEMSET(Sf[p], 0.0)
              for j in range(NCORES):
                  t = f256.next()
                  TS("dve", t, TV(g1_ap[:, j, :], g1b), msk[:, j:j + 1], None, ALU.mult, xr=[wreg])
                  STT(Sf[p], Sf[p], dm[:, j:j + 1], t, ALU.mult, ALU.add)
              CP("act", Sbf[p][0:64, 0:128], Sf[p][0:64, 0:128])
              CP("act", Sbf[p][64:128, 128:256], Sf[p][64:128, 128:256])
          CK(4.4)
          MEMSET(hin, 0.0)
          for j in range(NCORES):
              pm = sm[4]
              TS("dve", pm[:, 0:4], GS(j, 6, 10), -1.0, msk[:, j:j + 1], ALU.add, ALU.mult)
              TS("dve", pm[:, 0:4], pm[:, 0:4], 1.0, None, ALU.add)
              hm = sm[5]
              TS("dve", hm[:, 0:4], GS(j, 2, 6), msk[:, j:j + 1], None, ALU.mult)
              TT_("dve", hin, hin, pm[:, 0:4], ALU.mult)
              TT_("dve", hin, hin, hm[:, 0:4], ALU.add)

          CK(5)
          GUARD([wreg])
          WG = wview(512)
          wgb2 = [Buf() for _ in range(KC)]
          load_cols(WG, wgb2, w_in_d[l], GC, GC + 512, 0)
          for i in range(NT):
              for hd in range(4):
                  p, a = hd // 2, hd % 2
                  pgt = psr.next()
                  for kc in range(KC):
                      MM(pgt, TV(WG[:, kc, 128 * hd:128 * hd + 128], wgb2[kc]), U[kc][i],
                         start=(kc == 0), stop=(kc == KC - 1), xr=[wreg])
                  sg = f512.next()
                  ACT(sg, pgt, AF.Sigmoid)
                  pcr = psr.next()
                  MM(pcr, Sbf[p][:, 128 * a:128 * a + 128], S2[p][i])
                  o = f512.next()
                  TT_("dve", o, pcr, S1[hd][i], ALU.add)
                  osq = b512.next()
                  ACT(osq, o, AF.Square)
                  pss = psr.next()
                  MM(pss, ones_bf, osq)
                  t = f512.next()
                  ACT(t, pss, AF.Ln, bias=epsc, scale=1.0 / 128.0)
                  ACT(t, t, AF.Exp, scale=-0.5)
                  STT(o, o, spc(SP_GN), t, ALU.mult, ALU.mult)
                  TT_("dve", o, o, sg, ALU.mult)
                  TT_("dve", S1[hd][i], o, pgt, ALU.mult)
          CK(6)
          GUARD([wreg])
          WY = wview(512)
          wyb = [Buf() for _ in range(KC)]
          load_cols(WY, wyb, w_in_d[l], YC, YC + 512, 0)
          for i in range(NT):
              for c in range(4):
                  py = psr.next()
                  for kc in range(KC):
                      MM(py, TV(WY[:, kc, 128 * c:128 * c + 128], wyb[kc]), U[kc][i],
                         start=(kc == 0), stop=(kc == KC - 1), xr=[wreg])
                  gy = f512.next()
                  ACT(gy, py, AF.Square, scale=GC3 ** 0.5)
                  STT(gy, gy, 1.0, py, ALU.add, ALU.mult)
                  ACT(gy, gy, AF.Sigmoid, scale=GC1)
                  hf = f512.next()
                  STT(hf, S4[c][i], hin[:, c:c + 1], S3[c][i], ALU.mult, ALU.add)
                  TT_("dve", hf, hf, gy, ALU.mult)
                  TT_("dve", S3[c][i], hf, py, ALU.mult)
          CK(7)
          for half in range(2):
              GUARD([wreg])
              WO = wview(512)
              wob = [Buf() for _ in range(KC)]
              load_cols(WO, wob, w_out_d[l], 512 * half, 512 * half + 512, 0)
              for i in range(NT):
                  for o4 in range(4):
                      oc = 4 * half + o4
                      po = psr.next()
                      for kk in range(KC):
                          src = S1[kk][i] if kk < 4 else S3[kk - 4][i]
                          MM(po, TV(WO[:, kk, 128 * o4:128 * o4 + 128], wob[kk]), src,
                             start=(kk == 0), stop=(kk == KC - 1), xr=[wreg])
                      TT_("dve", H[oc][i], H[oc][i], po, ALU.add)

          CK(8)
          halo_exchange(2, lambda kc: spc(SP_LNF + kc))
          CK(9)
          for i in range(NT):
              rmsnorm(lambda kc: H[kc][i], lambda kc: spc(SP_LNF + kc), lambda kc: U[kc][i], TT)
          CK(10)
          MEMSET(dummy2, 0.0, eng="dve", xw=[wreg])
          SQ3 = GC3 ** 0.5

          def ffn_load(g):
              fin, fdn = WF[g % 2]
              gd = wf_guard[g % 2]
              fa_b = [Buf() for _ in range(KC)]
              fg_b = [Buf() for _ in range(KC)]
              fd_b = [Buf() for _ in range(FG)]
              GUARD([gd] + (store_bufs if g < 2 else []))
              load_cols(fin, fa_b, f_in_d[l], 512 * g, 512 * g + 512, 0)
              load_cols(fin, fg_b, f_in_d[l], FH + 512 * g, FH + 512 * g + 512, 512)
              for j in range(FG):
                  DMA("pool", fdn[:, j, :], f_dn_d[l][512 * g + 128 * j:512 * g + 128 * j + 128, :], [], [fd_b[j]])
              return (fin, fdn, gd, fa_b, fg_b, fd_b)

          def stage_a1(W, g, i, jj):
              fin, fdn, gd, fa_b, fg_b, fd_b = W
              zss = []
              for half in range(2):
                  z = (24 * half) + FG * g + jj
                  wb = fa_b if half == 0 else fg_b
                  col = 512 * half + 128 * jj
                  pz = psf.next()
                  for kc in range(KC):
                      MM(pz, TV(fin[:, kc, col:col + 128], wb[kc]), U[kc][i],
                         start=(kc == 0), stop=(kc == KC - 1), xr=[gd])
                  zs = fzr.next()
                  CP("act", zs[:, 2:514], pz)
                  if i == 0:
                      ph = psf.next()
                      for kc in range(KC):
                          MM(ph[:, 0:2], TV(fin[:, kc, col:col + 128], wb[kc]), TV(uh.ap[:, kc, 0:2], uh.buf),
                             start=(kc == 0), stop=(kc == KC - 1), xr=[gd])
                      CP("act", zs[:, 0:2], ph[:, 0:2])
                  else:
                      CP("pool", zs[:, 0:2], ztail[z])
                  if i < NT - 1:
                      CP("pool", ztail[z], zs[:, 512:514])
                  zss.append((z, zs))
              return zss

          def stage_a2(zss):
              accs = [ffr.next() for _ in zss]
              fw = lambda z, j: spc(SP_FCW + 3 * z + j)
              for (z, zs), acc in zip(zss, accs):
                  ACT(acc, zs[:, 2:514], AF.Identity, bias=spc(SP_FCB + z), scale=fw(z, 2))
              for tap in (1, 0):
                  for (z, zs), acc in zip(zss, accs):
                      STT(acc, zs[:, tap:tap + 512], fw(z, tap), acc, ALU.mult, ALU.add)
              return accs

          def stage_b1a(accs):
              x2 = ffr.next()
              ACT(x2, accs[0], AF.Square, scale=SQ3)
              STT(x2, x2, 1.0, accs[0], ALU.add, ALU.mult)
              return x2

          def stage_b2(accs, x2):
              ACT(x2, x2, AF.Sigmoid, scale=GC1)
              TT_("pool", x2, x2, accs[0], ALU.mult)
              ab = b512.next()
              TT_("pool", ab, x2, accs[1], ALU.mult)
              return ab

          def down(W, i, acts):
              fin, fdn, gd, fa_b, fg_b, fd_b = W
              for oc in range(KC):
                  pd = psf.next()
                  for jj in range(FG):
                      MM(pd, TV(fdn[:, jj, 128 * oc:128 * oc + 128], fd_b[jj]), acts[jj],
                         start=(jj == 0), stop=(jj == FG - 1), xr=[gd])
                  TT_("dve", H[oc][i], H[oc][i], pd, ALU.add)

          units = [(g, i, jj) for g in range(NG) for i in range(NT) for jj in range(FG)]
          NU = len(units)
          Wg = {0: ffn_load(0)}
          st = {}
          acts = []
          pend_down = None
          for k in range(NU + 5):
              if 0 <= k - 1 < NU:
                  st[k - 1]["accs"] = stage_a2(st[k - 1]["zss"])
              if 0 <= k - 2 < NU:
                  st[k - 2]["x2"] = stage_b1a(st[k - 2]["accs"])
              new_down = None
              if 0 <= k - 3 < NU:
                  s3 = st.pop(k - 3)
                  acts.append(stage_b2(s3["accs"], s3["x2"]))
                  if s3["jj"] == FG - 1:
                      new_down = (s3["W"], s3["i"], acts)
                      acts = []
              if k < NU:
                  g, i, jj = units[k]
                  st[k] = {"W": Wg[g], "i": i, "jj": jj}
                  st[k]["zss"] = stage_a1(Wg[g], g, i, jj)
              if pend_down is not None:
                  down(*pend_down)
              pend_down = new_down
              if k < NU:
                  g, i, jj = units[k]
                  if i == 1 and jj == 2 and g + 1 < NG:
                      Wg[g + 1] = ffn_load(g + 1)
          assert pend_down is None and not acts
          MEMSET(dummy2, 0.0, eng="dve", xw=[wreg] + ffx_bufs)

    except _Stop:
        pass

    finals = []
    if final_norm:
        for i in range(NT):
            ss = psr.next()
            for kc in range(KC):
                sq = b512.next()
                ACT(sq, H[kc][i], AF.Square)
                MM(ss, ones_bf, sq, start=(kc == 0), stop=(kc == KC - 1))
            t = f512.next()
            ACT(t, ss, AF.Ln, bias=epsc, scale=1.0 / D)
            ACT(t, t, AF.Exp, scale=-0.5)
            for kc in range(KC):
                STT(H[kc][i], H[kc][i], cst[:, CS_LNF + kc:CS_LNF + kc + 1], t, ALU.mult, ALU.mult)
                finals.append(DMA("sp", hout_d[:, kc, tsl(i)], H[kc][i].ap, [H[kc][i]], []))
    else:
        for i in range(NT):
            for kc in range(KC):
                finals.append(DMA("sp", hout_d[:, kc, tsl(i)], H[kc][i].ap, [H[kc][i]], []))
    kb.emit(final_ops=finals)
    return nc, es


def _fm(a):
    t = a.shape[0]
    return np.ascontiguousarray(a.reshape(t, KC, 128).transpose(2, 1, 0))


def _pack_params(inp, layers):
    L = len(layers)
    spar = np.zeros((L, 128, NSP), np.float32)
    lruw = np.zeros((L, 128, 1024), np.float32)
    w2a = np.zeros((L, 32, 256), np.float32)
    for li, l in enumerate(layers):
        spar[li, :, SP_LNM:SP_LNM + 8] = inp["ln_mix"][l].reshape(8, 128).T
        spar[li, :, SP_LNF:SP_LNF + 8] = inp["ln_ffn"][l].reshape(8, 128).T
        spar[li, :, SP_GN] = inp["gla_norm"][l]
        cw = inp["lru_conv_w"][l].reshape(4, 4, 128)
        spar[li, :, SP_LCW:SP_LCW + 16] = cw.transpose(2, 1, 0).reshape(128, 16)
        spar[li, :, SP_LCB:SP_LCB + 4] = inp["lru_conv_b"][l].reshape(4, 128).T
        spar[li, :, SP_BA:SP_BA + 4] = inp["lru_ba"][l].reshape(4, 128).T
        spar[li, :, SP_BX:SP_BX + 4] = inp["lru_bx"][l].reshape(4, 128).T
        spar[li, :, SP_LAM:SP_LAM + 4] = inp["lru_lambda"][l].reshape(4, 128).T
        fw = inp["ffn_conv_w"][l].reshape(3, 48, 128)
        spar[li, :, SP_FCW:SP_FCW + 144] = fw.transpose(2, 1, 0).reshape(128, 144)
        spar[li, :, SP_FCB:SP_FCB + 48] = inp["ffn_conv_b"][l].reshape(48, 128).T
        for c in range(4):
            for gi, key in enumerate(("lru_wa", "lru_wx")):
                blk = np.zeros((128, 128), np.float32)
                blk[0:64, 0:64] = inp[key][l][2 * c]
                blk[64:128, 64:128] = inp[key][l][2 * c + 1]
                lruw[li, :, (2 * c + gi) * 128:(2 * c + gi + 1) * 128] = blk
        w2a[li, 0:16, :] = inp["gla_gate_w2"][l]
        w2a[li, 16, :] = inp["gla_gate_b"][l]
    return spar, lruw, w2a


def _consts(c, ln_final):
    cst = np.zeros((128, NCST), np.float32)
    s = np.arange(128)[:, None]
    t = np.arange(128)[None, :]
    cst[:, CS_TRI:CS_TRI + 128] = (s > t)
    m = (s <= t).astype(np.float32)
    cst[:, CS_MASK:CS_MASK + 128] = m
    cst[:, CS_MASK + 128:CS_MASK + 256] = m
    cst[:, CS_MSK:CS_MSK + 8] = (np.arange(8) < c)[None, :]
    cst[:, CS_SEL:CS_SEL + 8] = (np.arange(8) == c - 1)[None, :]
    cst[:, CS_FF] = 1.0 if c == 0 else 0.0
    cst[:, CS_FF + 1] = 0.0 if c == 0 else 1.0
    cst[:, CS_LNF:CS_LNF + 8] = ln_final.reshape(8, 128).T
    return cst


_PROG = {}


def _get_prog(L, final_norm):
    key = (L, final_norm)
    if key not in _PROG:
        _PROG[key] = build_program(L, final_norm)
    return _PROG[key][0]


def _run(h_fm_list, halo_list, inp, layers, final_norm):
    L = len(layers)
    nc = _get_prog(L, final_norm)
    spar, lruw, w2a = _pack_params(inp, layers)
    sl = slice(layers[0], layers[-1] + 1)
    w_in = np.ascontiguousarray(inp["w_in"][sl])
    w_out = np.ascontiguousarray(inp["w_out"][sl])
    f_in = np.ascontiguousarray(inp["ffn_w_in"][sl])
    f_dn = np.ascontiguousarray(inp["ffn_w_down"][sl])
    in_maps = []
    for c in range(NCORES):
        in_maps.append({
            "hin": h_fm_list[c], "halo": halo_list[c], "w_in": w_in, "w_out": w_out, "f_in": f_in, "f_dn": f_dn,
            "spar": spar, "lruw": lruw, "w2a": w2a, "cst": _consts(c, inp["ln_final"]),
        })
    res = run_bass_kernel_spmd(nc, in_maps, core_ids=list(range(NCORES)))
    return [r["hout"] for r in res.results]


def _halos(h_fm_list):
    out = []
    for c in range(NCORES):
        if c == 0:
            out.append(np.zeros((128, KC * 3), np.float32))
        else:
            out.append(np.ascontiguousarray(h_fm_list[c - 1][:, :, T - 3:T].reshape(128, KC * 3)))
    return out


FUSED = True


def kernel(**inputs):
    inp = {k: np.asarray(v, dtype=np.float32) for k, v in inputs.items()}
    x = inp["x"][0]
    h = [_fm(x[c * T:(c + 1) * T]) for c in range(NCORES)]
    if FUSED:
        h = _run(h, _halos(h), inp, list(range(DEPTH)), True)
    else:
        for l in range(DEPTH):
            h = _run(h, _halos(h), inp, [l], l == DEPTH - 1)
    out = np.concatenate([o.transpose(2, 1, 0).reshape(T, D) for o in h], axis=0)
    return out[None].astype(np.float32)
```
